# Optimizing a Trainium2 kernel written in Bass

```python
import jax, jax.numpy as jnp
from jax import lax
import numpy as np

D_MODEL = 1024
BATCH = 1
SEQ = 16384
DEPTH = 1

GM_WIDTH = D_MODEL
GM_GROUPS = 8
GM_GROUP_DIM = GM_WIDTH // GM_GROUPS
GM_CHUNK = 128
N_HEADS = 16
HEAD_DIM = 64
ATT_WIDTH = N_HEADS * HEAD_DIM
MOBA_BLOCK = 256
MOBA_TOPK = 3
Q_BLOCK = 64
ROPE_THETA = 500000.0
ROT_DIM = HEAD_DIM // 4
D_FF = 2816
LN_EPS = 1e-5
DEEPNORM_ALPHA = (2.0 * DEPTH) ** 0.25
DEEPNORM_BETA = (8.0 * DEPTH) ** -0.25
SPLITS = (GM_WIDTH, GM_WIDTH, ATT_WIDTH, ATT_WIDTH, ATT_WIDTH, D_MODEL, D_MODEL)
IN_COLS = sum(SPLITS)

kernel_name = "hybrid_gmlp_moba_macaron_deepnorm"


def layer_norm(x, g, b):
    xf = x.astype(jnp.float32)
    mu = jnp.mean(xf, axis=-1, keepdims=True)
    var = jnp.mean(jnp.square(xf - mu), axis=-1, keepdims=True)
    y = (xf - mu) * lax.rsqrt(var + LN_EPS)
    return (y * g.astype(jnp.float32) + b.astype(jnp.float32)).astype(x.dtype)


def swiglu(x, w_gate, w_up, w_down):
    return (jax.nn.silu(x @ w_gate) * (x @ w_up)) @ w_down


def partial_rotary(x, pos):
    inv_freq = ROPE_THETA ** (-jnp.arange(0, ROT_DIM, 2, dtype=jnp.float32) / ROT_DIM)
    ang = pos.astype(jnp.float32)[:, None] * inv_freq[None, :]
    cos = jnp.cos(ang)[None, :, None, :]
    sin = jnp.sin(ang)[None, :, None, :]
    xr = x[..., :ROT_DIM].astype(jnp.float32)
    x1, x2 = xr[..., : ROT_DIM // 2], xr[..., ROT_DIM // 2:]
    rot = jnp.concatenate([x1 * cos - x2 * sin, x2 * cos + x1 * sin], axis=-1)
    return jnp.concatenate([rot.astype(x.dtype), x[..., ROT_DIM:]], axis=-1)


def gmlp_spatial_gating(u, v, ln_g, ln_b, w_s, b_s):
    B, S, _ = v.shape
    nc = S // GM_CHUNK
    vn = layer_norm(v, ln_g, ln_b).reshape(B, nc, GM_CHUNK, GM_GROUPS, GM_GROUP_DIM)
    causal = jnp.tril(jnp.ones((GM_CHUNK, GM_CHUNK), dtype=bool))
    ws = jnp.where(causal[None], w_s, 0)
    sv = jnp.einsum('gts,bnsgc->bntgc', ws, vn) + b_s.T[None, None, :, :, None]
    return u * sv.reshape(B, S, GM_WIDTH)


def moba_attention(q, k, v):
    B, S, H, dh = q.shape
    sp = -(-S // MOBA_BLOCK) * MOBA_BLOCK
    pad = ((0, 0), (0, sp - S), (0, 0), (0, 0))
    q, k, v = (jnp.pad(t, pad).transpose(0, 2, 1, 3) for t in (q, k, v))
    nb = sp // MOBA_BLOCK
    topk = min(MOBA_TOPK, nb - 1)
    kb = k.reshape(B, H, nb, MOBA_BLOCK, dh)
    vb = v.reshape(B, H, nb, MOBA_BLOCK, dh)
    kmean = jnp.mean(kb.astype(jnp.float32), axis=3)
    scale = dh ** -0.5
    nqb = sp // Q_BLOCK
    qblocks = q.reshape(B, H, nqb, Q_BLOCK, dh).transpose(2, 0, 1, 3, 4)
    bi = jnp.arange(B)[:, None, None, None]
    hi = jnp.arange(H)[None, :, None, None]

    def step(args):
        qi, qb_idx = args
        q_start = qb_idx * Q_BLOCK
        own = q_start // MOBA_BLOCK
        qpos = q_start + jnp.arange(Q_BLOCK)
        kpos = own * MOBA_BLOCK + jnp.arange(MOBA_BLOCK)
        k_own = lax.dynamic_index_in_dim(kb, own, axis=2, keepdims=False)
        v_own = lax.dynamic_index_in_dim(vb, own, axis=2, keepdims=False)
        s_own = jnp.einsum('bhqd,bhkd->bhqk', qi, k_own,
                           preferred_element_type=jnp.float32) * scale
        s_own = jnp.where(kpos[None, :] <= qpos[:, None], s_own, -jnp.inf)
        if topk > 0:
            gate = jnp.einsum('bhqd,bhnd->bhqn', qi.astype(jnp.float32), kmean)
            gate = jnp.where(jnp.arange(nb) < own, gate, -jnp.inf)
            _, sel = lax.top_k(gate, topk)
            valid = sel < own
            k_sel = kb[bi, hi, sel]
            v_sel = vb[bi, hi, sel]
            s_sel = jnp.einsum('bhqd,bhqjkd->bhqjk', qi, k_sel,
                               preferred_element_type=jnp.float32) * scale
            s_sel = jnp.where(valid[..., None], s_sel, -jnp.inf)
            s_all = jnp.concatenate([s_sel.reshape(B, H, Q_BLOCK, topk * MOBA_BLOCK), s_own], axis=-1)
            p = jax.nn.softmax(s_all, axis=-1).astype(v.dtype)
            p_sel = p[..., : topk * MOBA_BLOCK].reshape(B, H, Q_BLOCK, topk, MOBA_BLOCK)
            p_own = p[..., topk * MOBA_BLOCK:]
            out = (jnp.einsum('bhqjk,bhqjkd->bhqd', p_sel, v_sel)
                   + jnp.einsum('bhqk,bhkd->bhqd', p_own, v_own))
        else:
            p_own = jax.nn.softmax(s_own, axis=-1).astype(v.dtype)
            out = jnp.einsum('bhqk,bhkd->bhqd', p_own, v_own)
        return out

    out = lax.map(step, (qblocks, jnp.arange(nqb)))
    out = out.transpose(1, 0, 3, 2, 4).reshape(B, sp, H, dh)
    return out[:, :S]


def setup_inputs(seed: int = 0) -> dict:
    key = jax.random.key(seed)
    ks = jax.random.split(key, 24)
    L = DEPTH
    f32 = jnp.float32

    def nrm(k, shape, scale):
        return jax.random.normal(k, shape, f32) * scale

    def gain(k, shape):
        return 1.0 + 0.02 * jax.random.normal(k, shape, f32)

    return {
        "x": jax.random.normal(ks[0], (BATCH, SEQ, D_MODEL), f32),
        "ffn1_w_gate": nrm(ks[1], (L, D_MODEL, D_FF), D_MODEL ** -0.5),
        "ffn1_w_up": nrm(ks[2], (L, D_MODEL, D_FF), D_MODEL ** -0.5),
        "ffn1_w_down": nrm(ks[3], (L, D_FF, D_MODEL), DEEPNORM_BETA * D_FF ** -0.5),
        "ln1_g": gain(ks[4], (L, D_MODEL)),
        "ln1_b": nrm(ks[5], (L, D_MODEL), 0.02),
        "w_in": nrm(ks[6], (L, D_MODEL, IN_COLS), D_MODEL ** -0.5),
        "gm_ln_g": gain(ks[7], (L, GM_WIDTH)),
        "gm_ln_b": nrm(ks[8], (L, GM_WIDTH), 0.02),
        "gm_w_s": nrm(ks[9], (L, GM_GROUPS, GM_CHUNK, GM_CHUNK), GM_CHUNK ** -0.5),
        "gm_b_s": gain(ks[10], (L, GM_GROUPS, GM_CHUNK)),
        "w_gm_out": nrm(ks[11], (L, GM_WIDTH, D_MODEL), DEEPNORM_BETA * GM_WIDTH ** -0.5),
        "w_att_out": nrm(ks[12], (L, ATT_WIDTH, D_MODEL), DEEPNORM_BETA * ATT_WIDTH ** -0.5),
        "w_o": nrm(ks[13], (L, D_MODEL, D_MODEL), DEEPNORM_BETA * D_MODEL ** -0.5),
        "ln2_g": gain(ks[14], (L, D_MODEL)),
        "ln2_b": nrm(ks[15], (L, D_MODEL), 0.02),
        "ffn2_w_gate": nrm(ks[16], (L, D_MODEL, D_FF), D_MODEL ** -0.5),
        "ffn2_w_up": nrm(ks[17], (L, D_MODEL, D_FF), D_MODEL ** -0.5),
        "ffn2_w_down": nrm(ks[18], (L, D_FF, D_MODEL), DEEPNORM_BETA * D_FF ** -0.5),
        "ln3_g": gain(ks[19], (L, D_MODEL)),
        "ln3_b": nrm(ks[20], (L, D_MODEL), 0.02),
    }


def reference(x, ffn1_w_gate, ffn1_w_up, ffn1_w_down, ln1_g, ln1_b, w_in,
              gm_ln_g, gm_ln_b, gm_w_s, gm_b_s, w_gm_out, w_att_out, w_o,
              ln2_g, ln2_b, ffn2_w_gate, ffn2_w_up, ffn2_w_down, ln3_g, ln3_b):
    B, S, _ = x.shape
    pos = jnp.arange(S, dtype=jnp.int32)
    offs = np.cumsum(SPLITS)[:-1].tolist()
    for l in range(DEPTH):
        x = layer_norm(DEEPNORM_ALPHA * x + 0.5 * swiglu(x, ffn1_w_gate[l], ffn1_w_up[l], ffn1_w_down[l]),
                       ln1_g[l], ln1_b[l])
        proj = x @ w_in[l]
        u, v_gm, q, k, v_att, g_gm, g_att = jnp.split(proj, offs, axis=-1)
        y_gm = gmlp_spatial_gating(jax.nn.gelu(u), jax.nn.gelu(v_gm), gm_ln_g[l], gm_ln_b[l],
                                   gm_w_s[l], gm_b_s[l]) @ w_gm_out[l]
        q = partial_rotary(q.reshape(B, S, N_HEADS, HEAD_DIM), pos)
        k = partial_rotary(k.reshape(B, S, N_HEADS, HEAD_DIM), pos)
        v_att = v_att.reshape(B, S, N_HEADS, HEAD_DIM)
        y_att = moba_attention(q, k, v_att).reshape(B, S, ATT_WIDTH) @ w_att_out[l]
        merged = jax.nn.sigmoid(g_gm) * y_gm + jax.nn.sigmoid(g_att) * y_att
        x = layer_norm(DEEPNORM_ALPHA * x + merged @ w_o[l], ln2_g[l], ln2_b[l])
        x = layer_norm(DEEPNORM_ALPHA * x + 0.5 * swiglu(x, ffn2_w_gate[l], ffn2_w_up[l], ffn2_w_down[l]),
                       ln3_g[l], ln3_b[l])
    return x
```

```python
import numpy as np
import ml_dtypes
from contextlib import ExitStack
import concourse.bass as bass
import concourse.mybir as mybir
from concourse.bass_utils import run_bass_kernel_spmd

F32 = mybir.dt.float32
BF16 = mybir.dt.bfloat16
AF = mybir.ActivationFunctionType
ALU = mybir.AluOpType
AX = mybir.AxisListType
ENG = ["pe", "act", "dve", "pool", "sp"]

NCORE = 8
S = 16384
T = 2048
NT = 16
D = 1024
DFF = 2816
NFC = 22
NH = 16
ALPHA = float(2.0 ** 0.25)
EPS = 1e-5
BIG = 30000.0
NEG = -1.0e30
ARENA = 207872


class Buf:
    __slots__ = ("w", "r", "name", "dsem", "excl")

    def __init__(self, name="", dsem=None, excl=False):
        self.w = None
        self.r = {}
        self.name = name
        self.dsem = dsem
        self.excl = excl


class DSem:
    __slots__ = ("sem", "count", "key")

    def __init__(self, sem, key):
        self.sem = sem
        self.count = 0
        self.key = key


class _Rec:
    def __init__(self):
        self.calls = []

    def __getattr__(self, name):
        def f(*a, **k):
            self.calls.append((name, a, k))
            return self
        return f


def _bind(fn):
    r = _Rec()
    fn(r)
    assert len(r.calls) == 1, r.calls
    name, a, k = r.calls[0]
    return lambda e: getattr(e, name)(*a, **k)


class Prog:
    def __init__(self, nc, stack):
        self.nc = nc
        self.stack = stack
        self.q = {e: [] for e in ENG}
        self.cnt = {e: 0 for e in ENG}
        self.pending = {e: False for e in ENG}
        self.known = {e: {} for e in ENG}
        self.sems = {e: stack.enter_context(nc.semaphore("s_" + e)) for e in ENG}
        self.semh = dict(self.sems)
        self.dsems = []
        self.trace = {e: [] for e in ENG}
        self.free = []
        self.inuse = []

    def dsem(self, persist=False):
        if not persist and self.free:
            d = self.free.pop()
            self.inuse.append(d)
            return d
        key = ("d", len(self.dsems))
        s = self.stack.enter_context(self.nc.semaphore("d%d" % len(self.dsems)))
        d = DSem(s, key)
        self.semh[key] = s
        self.dsems.append(d)
        if not persist:
            self.inuse.append(d)
        return d

    def bufs(self, n, name="", dma=False, excl=False, persist=False):
        return [Buf("%s%d" % (name, i), self.dsem(persist) if dma else None, excl) for i in range(n)]

    def _wait(self, eng, tok):
        if tok is None:
            return
        k, v = tok
        if self.known[eng].get(k, 0) >= v:
            return
        self.known[eng][k] = v
        self.trace[eng].append(('wait', k, v))
        sem = self.semh[k]
        self.q[eng].append(lambda e, sem=sem, v=v: e.wait_ge(sem, v))

    def _deps(self, eng, reads, writes):
        toks = []
        for b in reads:
            if b.w is not None:
                toks.append((b.w, "raw"))
        for b in writes:
            if b.w is not None:
                toks.append((b.w, "waw"))
            for k, v in b.r.items():
                toks.append(((k, v), "war"))
        for tok, kind in toks:
            k, v = tok
            if k == eng:
                if eng == "pe":
                    continue
                if kind == "war":
                    continue
                if v > self.cnt[eng]:
                    continue
            self._wait(eng, tok)

    def _commit(self, tok, reads, writes):
        for b in writes:
            b.w = tok
            b.r = {}
        for b in reads:
            if b in writes:
                continue
            k, v = tok
            if b.r.get(k, 0) < v:
                b.r[k] = v

    def op(self, eng, fn, reads=(), writes=(), signal=True):
        if any(b.excl for b in reads):
            writes = list(writes) + [b for b in reads if b.excl and b not in writes]
            reads = [b for b in reads if not b.excl]
        self._deps(eng, reads, writes)
        fn = _bind(fn)
        if signal:
            self.cnt[eng] += 1
            sem = self.sems[eng]
            self.q[eng].append(lambda e, fn=fn, sem=sem: fn(e).then_inc(sem, 1))
            self.pending[eng] = False
            tok = (eng, self.cnt[eng])
        else:
            self.q[eng].append(lambda e, fn=fn: fn(e))
            self.pending[eng] = True
            tok = (eng, self.cnt[eng] + 1)
        self.trace[eng].append(('op', tok, signal, [b.name for b in reads], [b.name for b in writes]))
        self._commit(tok, reads, writes)
        return tok

    def dma(self, eng, out_ap, in_ap, reads=(), writes=(), ds=None, inc=16, fn=None):
        if ds is None:
            for b in writes:
                if b.dsem is not None:
                    ds = b.dsem
                    break
        assert ds is not None
        self._deps(eng, reads, writes)
        ds.count += inc
        sem = ds.sem
        if fn is None:
            self.q[eng].append(
                lambda e, o=out_ap, i=in_ap, sem=sem, inc=inc: e.dma_start(out=o, in_=i).then_inc(sem, inc))
        else:
            fn = _bind(fn)
            self.q[eng].append(lambda e, fn=fn, sem=sem, inc=inc: fn(e).then_inc(sem, inc))
        tok = (ds.key, ds.count)
        self.trace[eng].append(('dma', tok, [b.name for b in reads], [b.name for b in writes]))
        self._commit(tok, reads, writes)
        return tok

    def barrier(self):
        for e in ENG:
            assert not self.pending[e], e
        for d in self.dsems:
            if d.count:
                self._wait("sp", (d.key, d.count))
        for o in ENG:
            if o != "sp" and self.cnt[o]:
                self._wait("sp", (o, self.cnt[o]))
        self.cnt["sp"] += 1
        sem = self.sems["sp"]
        self.q["sp"].append(lambda e, sem=sem: e.nop().then_inc(sem, 1))
        tok = ("sp", self.cnt["sp"])
        for e in ENG:
            if e != "sp":
                self._wait(e, tok)
            for o in ENG:
                self.known[e][o] = max(self.known[e].get(o, 0), self.cnt[o])
            for d in self.dsems:
                self.known[e][d.key] = max(self.known[e].get(d.key, 0), d.count)
        self.free.extend(self.inuse)
        self.inuse = []

    def final_wait(self, toks):
        for t in toks:
            self._wait("sp", t)

    def emit(self):
        nc = self.nc
        q = self.q
        with nc.Block() as block:
            @block.tensor
            def _(e):
                for f in q["pe"]:
                    f(e)

            @block.scalar
            def _(e):
                for f in q["act"]:
                    f(e)

            @block.vector
            def _(e):
                for f in q["dve"]:
                    f(e)

            @block.gpsimd
            def _(e):
                for f in q["pool"]:
                    f(e)

            @block.sync
            def _(e):
                for f in q["sp"]:
                    f(e)


class Arena:
    def __init__(self, t):
        self.t = t
        self.off = 0

    def alloc(self, shape, dt):
        n = int(np.prod(shape))
        nb = n * (4 if dt == F32 else 2)
        nb = (nb + 63) // 64 * 64
        assert self.off + nb <= ARENA, (self.off, nb)
        ap = self.t[:, self.off // 2:(self.off + nb) // 2]
        self.off += nb
        if dt == F32:
            ap = ap.bitcast(F32)
        ap = ap[:, 0:n]
        if len(shape) == 2:
            ap = ap.rearrange("p (a b) -> p a b", a=shape[0])
        elif len(shape) == 3:
            ap = ap.rearrange("p (a b c) -> p a b c", a=shape[0], b=shape[1])
        return ap


def blocks_of(c):
    out = []
    for m in range(4):
        out.append(16 * m + c)
        out.append(16 * m + 15 - c)
    return out


def block_loc(j):
    m, t = divmod(j, 16)
    if t < 8:
        return t, 2 * m
    return 15 - t, 2 * m + 1


def build_program(stage=3):
    nc = bass.Bass("TRN2", target_bir_lowering=False)

    def din(name, shape, dt=F32):
        return nc.dram_tensor(name, shape, dt, kind="ExternalInput").ap()

    def dint(name, shape, dt):
        return nc.dram_tensor(name, shape, dt).ap()

    x_d = din("x", [T, D])
    wg_d = [din("wg1", [D, DFF]), din("wg2", [D, DFF])]
    wu_d = [din("wu1", [D, DFF]), din("wu2", [D, DFF])]
    wd_d = [din("wd1", [DFF, D]), din("wd2", [DFF, D])]
    lng_d = [din("ln1g", [D]), din("ln2g", [D]), din("ln3g", [D])]
    lnb_d = [din("ln1b", [D]), din("ln2b", [D]), din("ln3b", [D])]
    win_d = din("win", [D, 7168])
    gmg_d, gmb_d = din("gmg", [D]), din("gmb", [D])
    gmws_d = din("gmws", [8, 128, 128])
    gmbs_d = din("gmbs", [1024])
    wgmo_d, wao_d, wo_d = din("wgmo", [D, D]), din("wao", [D, D]), din("wo", [D, D])
    cos_d, sin_d = din("cosT", [128, T]), din("sinT", [128, T])
    pmat_d = din("pmat", [128, 128])
    tri_d = din("trimask", [128, 128])
    cb_d = din("cb", [128, 512])
    sel_d = din("sel", [64, S])
    biasA_d, biasB_d = din("biasA", [128, 512]), din("biasB", [128, 512])
    identb_d = din("identb", [128, 128])
    identf_d = din("identf", [128, 128])
    out_d = nc.dram_tensor("out", [T, D], F32, kind="ExternalOutput").ap()

    x1_d = dint("x1_d", [T, D], F32)
    x2_d = dint("x2_d", [T, D], F32)
    q_d = dint("q_d", [1024, T], BF16)
    mb_d = dint("mb_d", [1024, T], BF16)
    agk_in_f = dint("agk_in", [1024, T // 2], F32)
    agv_in_f = dint("agv_in", [T, 512], F32)
    agk_out_f = dint("agk_out", [8 * 1024, T // 2], F32)
    agv_out_f = dint("agv_out", [8 * T, 512], F32)
    agk_in, agv_in, agk_out, agv_out = (a.bitcast(BF16) for a in (agk_in_f, agv_in_f, agk_out_f, agv_out_f))
    print("bitcast shapes", agk_in.shape, agv_out.shape)
    km_in = dint("km_in", [128, 64], F32)
    km_out = dint("km_out", [8 * 128, 64], F32)

    with ExitStack() as st:
        P = Prog(nc, st)
        arena_t = st.enter_context(nc.sbuf_tensor("arena", [128, ARENA // 2], BF16))
        pbank = [st.enter_context(nc.psum_tensor("pb%d" % i, [128, 512], F32)) for i in range(8)]
        PB = P.bufs(8, "pb", excl=True)
        A = Arena(arena_t)

        def pbf(i):
            return pbank[i][:].bitcast(BF16)

        xT = A.alloc([8, T], BF16)
        XT = P.bufs(2, "xT")
        identb = A.alloc([128], BF16)
        Bident = Buf("ident", P.dsem(True))
        P.dma("pool", identb, identb_d, writes=[Bident])
        warm_t = A.alloc([64], F32)
        Bwarm = Buf('warm')
        base_off = A.off

        out_toks = []

        def transpose_into_xT(xb_ap, Bxb, t, bank):
            pv = pbf(bank)
            for c in range(8):
                P.op("pe", lambda e, c=c: e.transpose(out=pv[:, c * 128:(c + 1) * 128], in_=xb_ap[:, c * 128:(c + 1) * 128], identity=identb),
                     reads=[Bxb, Bident], writes=[PB[bank]], signal=(c == 7))
            P.op("dve", lambda e: e.tensor_copy(out=xT[:, :, t * 128:(t + 1) * 128], in_=pv.rearrange("p (a b) -> p a b", a=8)),
                 reads=[PB[bank]], writes=[XT[t // 8]])

        def rstd_newton(st, Bst, var_c, out_c, tmp_c):
            v = st[:, tmp_c:tmp_c + 1]
            t = st[:, tmp_c + 1:tmp_c + 2]
            y = st[:, out_c:out_c + 1]
            P.op("dve", lambda e: e.tensor_scalar(out=v, in0=st[:, var_c:var_c + 1], scalar1=EPS, scalar2=None, op0=ALU.add), reads=[Bst], writes=[Bst])
            P.op("dve", lambda e: e.tensor_scalar(out=t, in0=v, scalar1=1.0, scalar2=None, op0=ALU.add), reads=[Bst], writes=[Bst])
            P.op("dve", lambda e: e.reciprocal(out=y, in_=t), reads=[Bst], writes=[Bst])
            for _ in range(14):
                P.op("dve", lambda e: e.tensor_tensor(out=t, in0=y, in1=y, op=ALU.mult), reads=[Bst], writes=[Bst])
                P.op("dve", lambda e: e.tensor_tensor(out=t, in0=t, in1=v, op=ALU.mult), reads=[Bst], writes=[Bst])
                P.op("dve", lambda e: e.tensor_scalar(out=t, in0=t, scalar1=-0.5, scalar2=1.5, op0=ALU.mult, op1=ALU.add), reads=[Bst], writes=[Bst])
                P.op("dve", lambda e: e.tensor_tensor(out=y, in0=y, in1=t, op=ALU.mult), reads=[Bst], writes=[Bst])

        def warm_act(func):
            P.barrier()
            P.op("dve", lambda e: e.memset(warm_t, 0.0), writes=[Bwarm])
            P.op("act", lambda e: e.activation(out=warm_t, in_=warm_t, func=func), reads=[Bwarm], writes=[Bwarm])
            P.barrier()

        class LNBufs:
            pass

        def alloc_ln():
            L = LNBufs()
            L.xtm = [A.alloc([D], F32) for _ in range(2)]
            L.Bxtm = P.bufs(2, "xtm", dma=True)
            L.z = A.alloc([D], F32)
            L.Bz = Buf("z")
            L.xo = [A.alloc([D], F32) for _ in range(2)]
            L.Bxo = P.bufs(2, "xo")
            L.xb = [A.alloc([D], BF16) for _ in range(2)]
            L.Bxb = P.bufs(2, "xb")
            L.gbc = A.alloc([D], F32)
            L.bbc = A.alloc([D], F32)
            L.Bgb = Buf("gb", P.dsem())
            L.st = A.alloc([20], F32)
            L.Bst = Buf("st")
            L.dst = [P.dsem(True) for _ in range(2)]
            L.n = 0
            return L

        def load_ln_params(L, li):
            P.dma("sp", L.gbc, lng_d[li].partition_broadcast(128), writes=[L.Bgb])
            P.dma("sp", L.bbc, lnb_d[li].partition_broadcast(128), writes=[L.Bgb])

        def ln_tile(L, t, ybanks, res_d, dst_d, tr_bank, final=False):
            i = L.n % 2
            L.n += 1
            P.dma("sp", L.xtm[i], res_d[t * 128:(t + 1) * 128, :], writes=[L.Bxtm[i]])
            for h in range(2):
                P.op("dve", lambda e, h=h: e.scalar_tensor_tensor(out=L.z[:, h * 512:(h + 1) * 512], in0=L.xtm[i][:, h * 512:(h + 1) * 512], scalar=ALPHA,
                                                                in1=pbank[ybanks[h]][:], op0=ALU.mult, op1=ALU.add),
                     reads=[L.Bxtm[i], PB[ybanks[h]]], writes=[L.Bz])
            for h in range(2):
                P.op("dve", lambda e, h=h: e.bn_stats(out=L.st[:, h * 6:(h + 1) * 6], in_=L.z[:, h * 512:(h + 1) * 512]), reads=[L.Bz], writes=[L.Bst])
            P.op("dve", lambda e: e.bn_aggr(out=L.st[:, 12:14], in_=L.st[:, 0:12]), reads=[L.Bst], writes=[L.Bst])
            rstd_newton(L.st, L.Bst, 13, 15, 16)
            P.op("dve", lambda e: e.tensor_scalar(out=L.z[:], in0=L.z[:], scalar1=L.st[:, 12:13], scalar2=L.st[:, 15:16], op0=ALU.subtract, op1=ALU.mult),
                 reads=[L.Bz, L.Bst], writes=[L.Bz])
            P.op("dve", lambda e: e.tensor_tensor(out=L.z[:], in0=L.z[:], in1=L.gbc, op=ALU.mult), reads=[L.Bz, L.Bgb], writes=[L.Bz])
            P.op("dve", lambda e: e.tensor_tensor(out=L.xo[i], in0=L.z[:], in1=L.bbc, op=ALU.add), reads=[L.Bz, L.Bgb], writes=[L.Bxo[i]])
            tok = P.dma("sp", dst_d[t * 128:(t + 1) * 128, :], L.xo[i], reads=[L.Bxo[i]], ds=L.dst[i])
            if final:
                out_toks.append(tok)
            else:
                P.op("act", lambda e: e.copy(out=L.xb[i], in_=L.xo[i]), reads=[L.Bxo[i]], writes=[L.Bxb[i]])
                transpose_into_xT(L.xb[i], L.Bxb[i], t, tr_bank)

        def ffn_phase(fi, li, res_d, dst_d, first, final):
            A.off = base_off
            warm_act(AF.Silu)
            hT = A.alloc([NFC, 1024], BF16)
            HT = P.bufs(NFC, "hT")
            wd = A.alloc([NFC, D], BF16)
            WD = P.bufs(NFC, "wd")
            dwd = P.dsem()
            for b_ in WD:
                b_.dsem = dwd
            wgu = [[A.alloc([8, 128], BF16) for _ in range(2)] for _ in range(3)]
            WGU = [P.bufs(2, "wgu", dma=True) for _ in range(3)]
            sg = [A.alloc([512], F32) for _ in range(4)]
            SG = P.bufs(4, "sg")
            L = alloc_ln()
            load_ln_params(L, li)
            if first:
                for t in range(NT):
                    i = t % 2
                    P.dma("sp", L.xtm[i], x_d[t * 128:(t + 1) * 128, :], writes=[L.Bxtm[i]])
                    P.op("act", lambda e, i=i: e.copy(out=L.xb[i], in_=L.xtm[i]), reads=[L.Bxtm[i]], writes=[L.Bxb[i]])
                    transpose_into_xT(L.xb[i], L.Bxb[i], t, t % 4)
            wgr = wg_d[fi].rearrange("(c p) f -> p c f", p=128)
            wur = wu_d[fi].rearrange("(c p) f -> p c f", p=128)
            for fc in range(NFC):
                P.dma("pool", wd[:, fc, :], wd_d[fi][fc * 128:(fc + 1) * 128, :], writes=[WD[fc]])
            nw = 0
            nsg = 0
            for stt in range(2):
                for fc in range(NFC):
                    w = nw % 3
                    nw += 1
                    P.dma("pool", wgu[w][0], wgr[:, :, fc * 128:(fc + 1) * 128], writes=[WGU[w][0]])
                    P.dma("pool", wgu[w][1], wur[:, :, fc * 128:(fc + 1) * 128], writes=[WGU[w][1]])
                    bset = 4 * (fc % 2)
                    for h in range(2):
                        cols = slice(stt * 1024 + h * 512, stt * 1024 + (h + 1) * 512)
                        for gu in range(2):
                            bank = bset + 2 * gu + h
                            for c in range(8):
                                P.op("pe", lambda e, bank=bank, c=c, w=w, gu=gu, cols=cols: e.matmul(pbank[bank][:], lhsT=wgu[w][gu][:, c, :], rhs=xT[:, c, cols], start=(c == 0), stop=(c == 7)),
                                     reads=[WGU[w][gu], XT[stt]], writes=[PB[bank]], signal=(c == 7))
                        si = nsg % 4
                        nsg += 1
                        P.op("act", lambda e, si=si, bank=bset + h: e.activation(out=sg[si], in_=pbank[bank][:], func=AF.Silu), reads=[PB[bset + h]], writes=[SG[si]])
                        P.op("dve", lambda e, si=si, bank=bset + 2 + h, fc=fc, h=h: e.scalar_tensor_tensor(out=hT[:, fc, h * 512:(h + 1) * 512], in0=pbank[bank][:], scalar=0.5, in1=sg[si],
                                                                                                   op0=ALU.mult, op1=ALU.mult),
                             reads=[PB[bset + 2 + h], SG[si]], writes=[HT[fc]])
                for tt in range(8):
                    t = stt * 8 + tt
                    yb = [4 * (tt % 2), 4 * (tt % 2) + 1]
                    for h in range(2):
                        for fc in range(NFC):
                            P.op("pe", lambda e, h=h, fc=fc, tt=tt, bank=yb[h]: e.matmul(pbank[bank][:], lhsT=hT[:, fc, tt * 128:(tt + 1) * 128], rhs=wd[:, fc, h * 512:(h + 1) * 512],
                                                                                     start=(fc == 0), stop=(fc == NFC - 1)),
                                 reads=[HT[fc], WD[fc]], writes=[PB[yb[h]]], signal=(fc == NFC - 1))
                    ln_tile(L, t, yb, res_d, dst_d, 4 * (tt % 2) + 2, final=final)
            P.barrier()

        ffn_phase(0, 0, x_d, x1_d if stage > 1 else out_d, True, stage == 1)

        if stage >= 2:
            mixer_phase(nc, P, A, base_off, locals())
        if stage >= 3:
            ffn_phase(1, 2, x2_d, out_d, False, True)

        P.final_wait(out_toks)
        P.emit()
    return nc


def mixer_phase(nc, P, A, base_off, env):
    import os
    g = env
    xT, XT, pbank, PB, pbf, identb, Bident = g["xT"], g["XT"], g["pbank"], g["PB"], g["pbf"], g["identb"], g["Bident"]
    win_d = g["win_d"]
    winr = win_d.rearrange("(c p) f -> p c f", p=128)
    A.off = base_off
    AT = A.alloc([8, T], BF16)
    BAT = P.bufs(4, "AT")
    attT = A.alloc([8, T], BF16)
    BattT = P.bufs(8, "attT", dma=True, persist=True)
    reg_off = A.off
    cosT = A.alloc([T], F32); sinT = A.alloc([T], F32)
    Bcs = Buf("cs", P.dsem())
    P.dma("sp", cosT, g["cos_d"], writes=[Bcs]); P.dma("sp", sinT, g["sin_d"], writes=[Bcs])
    pmat = A.alloc([128], BF16); Bpm = Buf("pm", P.dsem())
    P.dma("pool", pmat, g["pmat_d"], writes=[Bpm])
    wst = [A.alloc([8, 128], BF16) for _ in range(3)]; WST = P.bufs(3, "wst", dma=True)
    wtm = A.alloc([8, 1024], BF16); Bwtm = Buf("wtm", P.dsem())
    kraw = [A.alloc([512], BF16) for _ in range(2)]; Bkraw = P.bufs(2, "kraw")
    t1 = [A.alloc([512], F32) for _ in range(2)]; Bt1 = P.bufs(2, "t1")
    t2 = [A.alloc([512], F32) for _ in range(2)]; Bt2 = P.bufs(2, "t2")
    krot = [A.alloc([512], F32) for _ in range(2)]; Bkrot = P.bufs(2, "krot")
    kbf = [A.alloc([512], BF16) for _ in range(2)]; Bkbf = P.bufs(2, "kbf"); dkbf = [P.dsem() for _ in range(2)]
    vsb = [A.alloc([1024], BF16) for _ in range(2)]; Bvsb = P.bufs(2, "vsb"); dvsb = [P.dsem() for _ in range(2)]
    kst = A.alloc([16], F32); Bkst = Buf("kst")
    kmloc = A.alloc([8, 8], F32); Bkmloc = Buf("kmloc")
    kmall2 = A.alloc([8, 64], F32); Bkm2 = Buf("km2", P.dsem())
    kmall = A.alloc([8, 64], F32); Bkmall = Buf("kmall")
    bA = A.alloc([512], F32); bB = A.alloc([512], F32); BbAB = Buf("bAB", P.dsem())
    P.dma("sp", bA, g["biasA_d"], writes=[BbAB]); P.dma("sp", bB, g["biasB_d"], writes=[BbAB])
    Gms = [A.alloc([64], F32) for _ in range(4)]; BGms = P.bufs(4, "Gm")
    mx8s = [A.alloc([8], F32) for _ in range(4)]; Bmxs = P.bufs(4, "mx")
    gcnt = [0]
    MBq = [A.alloc([128], BF16) for _ in range(2)]; BMBq = P.bufs(2, "MBq")
    mbT = [A.alloc([512], BF16) for _ in range(2)]; BmbT = P.bufs(2, "mbT"); dmbT = [P.dsem() for _ in range(2)]
    for i in range(2):
        P.op("dve", lambda e, i=i: e.memset(MBq[i], 0.0), writes=[BMBq[i]])
    nws = [0]
    Bagk = Buf("agk", P.dsem()); Bagv = Buf("agv", P.dsem()); Bkmin = Buf("kmin", P.dsem())
    Bagko = Buf("agko", P.dsem()); Bagvo = Buf("agvo", P.dsem()); Bkmo = Buf("kmo", P.dsem())
    Bqd = Buf("qd"); Bmbd = Buf("mbd")
    grp = [list(range(NCORE))]

    def proj_fm(col0, bank, tt):
        w = nws[0] % 3
        return w

    def load_w(col0):
        w = nws[0] % 3
        nws[0] += 1
        P.dma("pool", wst[w], winr[:, :, col0:col0 + 128], writes=[WST[w]])
        return w

    def fm_mm(w, bank, tt):
        for c in range(8):
            P.op("pe", lambda e, c=c: e.matmul(pbank[bank][:], lhsT=wst[w][:, c, :], rhs=xT[:, c, tt * 512:(tt + 1) * 512], start=(c == 0), stop=(c == 7)),
                 reads=[WST[w], XT[tt // 2]], writes=[PB[bank]], signal=(c == 7))

    def rope(bank, bank2, tt, n, scale):
        i = n % 2
        P.op("act", lambda e: e.copy(out=kraw[i], in_=pbank[bank][:]), reads=[PB[bank]], writes=[Bkraw[i]])
        P.op("pe", lambda e: e.matmul(pbank[bank2][:], lhsT=pmat, rhs=kraw[i], start=True, stop=True), reads=[Bpm, Bkraw[i]], writes=[PB[bank2]])
        cs = slice(tt * 512, (tt + 1) * 512)
        P.op("dve", lambda e: e.tensor_tensor(out=t1[i], in0=pbank[bank][:], in1=cosT[:, cs], op=ALU.mult), reads=[PB[bank], Bcs], writes=[Bt1[i]])
        P.op("dve", lambda e: e.tensor_tensor(out=t2[i], in0=pbank[bank2][:], in1=sinT[:, cs], op=ALU.mult), reads=[PB[bank2], Bcs], writes=[Bt2[i]])
        P.op("dve", lambda e: e.tensor_tensor(out=krot[i], in0=t1[i], in1=t2[i], op=ALU.add), reads=[Bt1[i], Bt2[i]], writes=[Bkrot[i]])
        P.op("act", lambda e: e.activation(out=kbf[i], in_=krot[i], func=AF.Copy, scale=scale), reads=[Bkrot[i]], writes=[Bkbf[i]])
        return i

    n = 0
    for pr in range(8):
        w = load_w(3072 + pr * 128)
        for tt in range(4):
            bank = (n % 2) * 2
            fm_mm(w, bank, tt)
            i = rope(bank, bank + 1, tt, n, 1.0)
            n += 1
            P.dma("sp", g["agk_in"][pr * 128:(pr + 1) * 128, tt * 512:(tt + 1) * 512], kbf[i], reads=[Bkbf[i]], writes=[Bagk], ds=dkbf[i])
            for hb in range(2):
                P.op("dve", lambda e, hb=hb, i=i: e.bn_stats(out=kst[:, 0:6], in_=krot[i][:, hb * 256:(hb + 1) * 256]), reads=[Bkrot[i]], writes=[Bkst])
                P.op("dve", lambda e, hb=hb, pr=pr, tt=tt: e.bn_aggr(out=kst[:, 8:10], in_=kst[:, 0:6]), reads=[Bkst], writes=[Bkst])
                P.op("dve", lambda e, hb=hb, pr=pr, tt=tt: e.tensor_copy(out=kmloc[:, pr, tt * 2 + hb:tt * 2 + hb + 1], in_=kst[:, 8:9]), reads=[Bkst], writes=[Bkmloc])
    P.dma("sp", g["km_in"], kmloc.rearrange("p a b -> p (a b)"), reads=[Bkmloc], writes=[Bkmin])
    NOCC = os.environ.get("NOCC", "0") == "1"
    if not NOCC:
      P.dma("pool", None, None, reads=[Bkmin], writes=[Bkmo], inc=1,
          fn=lambda e: e.collective_compute("AllGather", ALU.bypass, replica_groups=grp, ins=[g["km_in"]], outs=[g["km_out"]]))
    if not NOCC:
      P.dma("pool", None, None, reads=[Bagk], writes=[Bagko], inc=1,
          fn=lambda e: e.collective_compute("AllGather", ALU.bypass, replica_groups=grp, ins=[g["agk_in_f"]], outs=[g["agk_out_f"]]))
    P.dma("pool", wtm, winr[:, :, 4096:5120], writes=[Bwtm])
    for t in range(NT):
        i = t % 2
        for h in range(2):
            bank = 4 + h
            for c in range(8):
                P.op("pe", lambda e, c=c, h=h, t=t, bank=bank: e.matmul(pbank[bank][:], lhsT=xT[:, c, t * 128:(t + 1) * 128], rhs=wtm[:, c, h * 512:(h + 1) * 512], start=(c == 0), stop=(c == 7)),
                     reads=[Bwtm, XT[t // 8]], writes=[PB[bank]], signal=(c == 7))
            P.op("act", lambda e, h=h, i=i, bank=bank: e.copy(out=vsb[i][:, h * 512:(h + 1) * 512], in_=pbank[bank][:]), reads=[PB[bank]], writes=[Bvsb[i]])
        P.dma("sp", g["agv_in"][t * 128:(t + 1) * 128, :], vsb[i], reads=[Bvsb[i]], writes=[Bagv], ds=dvsb[i])
    if not NOCC:
      P.dma("pool", None, None, reads=[Bagv], writes=[Bagvo], inc=1,
          fn=lambda e: e.collective_compute("AllGather", ALU.bypass, replica_groups=grp, ins=[g["agv_in_f"]], outs=[g["agv_out_f"]]))
    P.dma("sp", kmall2.rearrange("p a b -> p (a b)").rearrange("p (r c) -> p r c", r=8), g["km_out"].rearrange("(r p) c -> p r c", p=128), reads=[Bkmo], writes=[Bkm2])
    for pr in range(8):
        P.op("dve", lambda e, pr=pr: e.tensor_copy(out=kmall[:, pr, :].rearrange("p (r i) -> p r i", r=8), in_=kmall2[:, :, pr * 8:(pr + 1) * 8]), reads=[Bkm2], writes=[Bkmall])
    for pr in range(8):
        w = load_w(2048 + pr * 128)
        for tt in range(4):
            bank = (n % 2) * 2
            fm_mm(w, bank, tt)
            i = rope(bank, bank + 1, tt, n, 0.125)
            n += 1
            P.dma("sp", g["q_d"][pr * 128:(pr + 1) * 128, tt * 512:(tt + 1) * 512], kbf[i], reads=[Bkbf[i]], writes=[Bqd], ds=dkbf[i])
            for par in range(2):
                hb = par * 64
                mi = (n * 2 + par) % 2
                for qt in range(4):
                    slot = (tt * 4 + qt) // 2
                    gc = gcnt[0]
                    gcnt[0] += 1
                    Gm, BGm, mx8, Bmx = Gms[gc % 4], BGms[gc % 4], mx8s[gc % 4], Bmxs[gc % 4]
                    gb = 4 + gc % 2
                    tb = 6 + gc % 2
                    P.op("pe", lambda e, hb=hb, qt=qt, i=i, pr=pr: e.matmul(pbank[gb][:, 0:64], lhsT=krot[i][hb:hb + 64, qt * 128:(qt + 1) * 128], rhs=kmall[hb:hb + 64, pr, :], start=True, stop=True),
                         reads=[Bkrot[i], Bkmall], writes=[PB[gb]])
                    P.op("dve", lambda e, slot=slot: e.tensor_tensor(out=Gm, in0=pbank[gb][:, 0:64], in1=bA[:, slot * 64:(slot + 1) * 64], op=ALU.add), reads=[PB[gb], BbAB], writes=[BGm])
                    P.op("dve", lambda e: e.max(out=mx8, in_=Gm), reads=[BGm], writes=[Bmx])
                    P.op("dve", lambda e: e.tensor_scalar(out=Gm, in0=Gm, scalar1=mx8[:, 2:3], scalar2=BIG, op0=ALU.is_ge, op1=ALU.mult), reads=[BGm, Bmx], writes=[BGm])
                    mq = qt % 2
                    P.op("dve", lambda e, slot=slot, mq=mq: e.tensor_tensor(out=MBq[mq][:, 64:128], in0=Gm, in1=bB[:, slot * 64:(slot + 1) * 64], op=ALU.add), reads=[BGm, BbAB], writes=[BMBq[mq]])
                    pv = pbf(tb)
                    P.op("pe", lambda e, mq=mq, pv=pv: e.transpose(out=pv[:, 0:128], in_=MBq[mq], identity=identb), reads=[BMBq[mq], Bident], writes=[PB[tb]])
                    P.op("act", lambda e, mi=mi, qt=qt, pv=pv: e.copy(out=mbT[mi][64:128, qt * 128:(qt + 1) * 128], in_=pv[64:128, 0:128]), reads=[PB[tb]], writes=[BmbT[mi]])
                h = pr * 2 + par
                P.dma("sp", g["mb_d"][h * 64:(h + 1) * 64, tt * 512:(tt + 1) * 512], mbT[mi][64:128, :], reads=[BmbT[mi]], writes=[Bmbd], ds=dmbT[mi])
    P.barrier()

    import os
    UPTO = os.environ.get("MIX_UPTO", "Z")
    if UPTO == "A":
        return
    A.off = reg_off
    g['warm_act'](AF.Gelu_apprx_tanh)
    wst = [A.alloc([8, 128], BF16) for _ in range(3)]
    wtm = A.alloc([8, 1024], BF16)
    wgmo = A.alloc([8, 1024], BF16); Bwgmo = Buf("wgmo", P.dsem())
    P.dma("pool", wtm, winr[:, :, 1024:2048], writes=[Bwtm])
    P.dma("pool", wgmo, g["wgmo_d"].rearrange("(c p) f -> p c f", p=128), writes=[Bwgmo])
    wsraw = A.alloc([8, 128], BF16); Bwsraw = Buf("wsraw", P.dsem())
    P.dma("pool", wsraw, g["gmws_d"].rearrange("g t s -> t g s"), writes=[Bwsraw])
    tri = A.alloc([128], F32); Btri = Buf("tri", P.dsem())
    P.dma("sp", tri, g["tri_d"], writes=[Btri])
    wsT = A.alloc([8, 128], BF16); BwsT = Buf("wsT")
    for gg in range(8):
        pv = pbf(7)
        P.op("pe", lambda e, gg=gg, pv=pv: e.transpose(out=pv[:, 0:128], in_=wsraw[:, gg, :], identity=identb), reads=[Bwsraw, Bident], writes=[PB[7]])
        P.op("dve", lambda e, gg=gg, pv=pv: e.tensor_tensor(out=wsT[:, gg, :], in0=pv[:, 0:128], in1=tri, op=ALU.mult), reads=[PB[7], Btri], writes=[BwsT])
    bsbc = A.alloc([8, 128], F32); gmgb = A.alloc([D], F32); gmbb = A.alloc([D], F32); Bgmp = Buf("gmp", P.dsem())
    P.dma("sp", bsbc.rearrange("p a b -> p (a b)"), g["gmbs_d"].partition_broadcast(128), writes=[Bgmp])
    P.dma("sp", gmgb, g["gmg_d"].partition_broadcast(128), writes=[Bgmp])
    P.dma("sp", gmbb, g["gmb_d"].partition_broadcast(128), writes=[Bgmp])
    ug = [A.alloc([8, 512], BF16) for _ in range(2)]; Bug = P.bufs(2, "ug")
    vg = A.alloc([D], F32); Bvg = Buf("vg")
    vn = [A.alloc([D], BF16) for _ in range(2)]; Bvn = P.bufs(2, "vn")
    gst = A.alloc([20], F32); Bgst = Buf("gst")
    svs = A.alloc([8, 128], F32); Bsvs = Buf("svs")
    mT = [A.alloc([8, 512], BF16) for _ in range(2)]; BmT = P.bufs(2, "mT")
    sgm = [A.alloc([512], F32) for _ in range(2)]; Bsgm = P.bufs(2, "sgm")
    nsg = 0
    for G in range(4):
        gi = G % 2
        for c8 in range(8):
            w = load_w(c8 * 128)
            bank = c8 % 2
            fm_mm(w, bank, G)
            P.op("act", lambda e, c8=c8, bank=bank: e.activation(out=ug[gi][:, c8, :], in_=pbank[bank][:], func=AF.Gelu_apprx_tanh), reads=[PB[bank]], writes=[Bug[gi]])
        for q4 in range(4):
            t = G * 4 + q4
            vi = t % 2
            for h in range(2):
                bank = 2 + h
                for c in range(8):
                    P.op("pe", lambda e, c=c, h=h, t=t, bank=bank: e.matmul(pbank[bank][:], lhsT=xT[:, c, t * 128:(t + 1) * 128], rhs=wtm[:, c, h * 512:(h + 1) * 512], start=(c == 0), stop=(c == 7)),
                         reads=[Bwtm, XT[t // 8]], writes=[PB[bank]], signal=(c == 7))
                P.op("act", lambda e, h=h, bank=bank: e.activation(out=vg[:, h * 512:(h + 1) * 512], in_=pbank[bank][:], func=AF.Gelu_apprx_tanh), reads=[PB[bank]], writes=[Bvg])
            for h in range(2):
                P.op("dve", lambda e, h=h: e.bn_stats(out=gst[:, h * 6:(h + 1) * 6], in_=vg[:, h * 512:(h + 1) * 512]), reads=[Bvg], writes=[Bgst])
            P.op("dve", lambda e: e.bn_aggr(out=gst[:, 12:14], in_=gst[:, 0:12]), reads=[Bgst], writes=[Bgst])
            g["rstd_newton"](gst, Bgst, 13, 15, 16)
            P.op("dve", lambda e: e.tensor_scalar(out=vg, in0=vg, scalar1=gst[:, 12:13], scalar2=gst[:, 15:16], op0=ALU.subtract, op1=ALU.mult), reads=[Bvg, Bgst], writes=[Bvg])
            P.op("dve", lambda e: e.tensor_tensor(out=vg, in0=vg, in1=gmgb, op=ALU.mult), reads=[Bvg, Bgmp], writes=[Bvg])
            P.op("dve", lambda e, vi=vi: e.tensor_tensor(out=vn[vi], in0=vg, in1=gmbb, op=ALU.add), reads=[Bvg, Bgmp], writes=[Bvn[vi]])
            for hb in range(2):
                bank = 4 + hb
                for g4 in range(4):
                    gg = hb * 4 + g4
                    P.op("pe", lambda e, gg=gg, g4=g4, vi=vi, bank=bank: e.matmul(pbank[bank][:, g4 * 128:(g4 + 1) * 128], lhsT=vn[vi][:, gg * 128:(gg + 1) * 128], rhs=wsT[:, gg, :], start=True, stop=True),
                         reads=[Bvn[vi], BwsT], writes=[PB[bank]], signal=(g4 == 3))
                P.op("dve", lambda e, hb=hb, bank=bank: e.tensor_tensor(out=svs[:, hb * 4:(hb + 1) * 4, :], in0=pbank[bank][:].rearrange("p (a b) -> p a b", a=4), in1=bsbc[:, hb * 4:(hb + 1) * 4, :], op=ALU.add),
                     reads=[PB[bank], Bgmp], writes=[Bsvs])
            P.op("dve", lambda e, q4=q4: e.tensor_tensor(out=mT[gi][:, :, q4 * 128:(q4 + 1) * 128], in0=svs, in1=ug[gi][:, :, q4 * 128:(q4 + 1) * 128], op=ALU.mult),
                 reads=[Bsvs, Bug[gi]], writes=[BmT[gi]])
        for dc in range(8):
            bank = 6
            for c in range(8):
                P.op("pe", lambda e, c=c, dc=dc: e.matmul(pbank[6][:], lhsT=wgmo[:, c, dc * 128:(dc + 1) * 128], rhs=mT[gi][:, c, :], start=(c == 0), stop=(c == 7)),
                     reads=[Bwgmo, BmT[gi]], writes=[PB[6]], signal=(c == 7))
            w = load_w(5120 + dc * 128)
            fm_mm(w, 7, G)
            si = nsg % 2
            nsg += 1
            P.op("act", lambda e, si=si: e.activation(out=sgm[si], in_=pbank[7][:], func=AF.Tanh, scale=0.5), reads=[PB[7]], writes=[Bsgm[si]])
            P.op("dve", lambda e, si=si: e.tensor_scalar(out=sgm[si], in0=sgm[si], scalar1=0.5, scalar2=0.5, op0=ALU.mult, op1=ALU.add), reads=[Bsgm[si]], writes=[Bsgm[si]])
            P.op("dve", lambda e, si=si, dc=dc: e.tensor_tensor(out=AT[:, dc, G * 512:(G + 1) * 512], in0=pbank[6][:], in1=sgm[si], op=ALU.mult), reads=[PB[6], Bsgm[si]], writes=[BAT[G]])
    P.barrier()

    if UPTO == "B":
        return
    print('CNT before attention', P.cnt)
    P.mark = {e: len(P.trace[e]) for e in ENG}
    A.off = reg_off
    if os.environ.get('EXP_WARM', '1') == '1':
        dmy = A.alloc([64], F32); Bdmy = Buf('dmy')
        P.op('dve', lambda e: e.memset(dmy, 0.0), writes=[Bdmy])
        P.op('act', lambda e: e.activation(out=dmy, in_=dmy, func=AF.Exp), reads=[Bdmy], writes=[Bdmy])
        P.barrier()
    KT = A.alloc([S], BF16)
    BKT = P.bufs(8, "KT", dma=True)
    Bsel = Buf("sel", P.dsem())
    P.dma("pool", KT[64:128, :], g["sel_d"], writes=[Bsel])
    VA = A.alloc([128, 66], BF16)
    BVA = P.bufs(8, "VA", dma=True)
    Bva1 = Buf("va1")
    P.op("dve", lambda e: e.memset(VA[:, :, 64:66], 1.0), writes=[Bva1])
    QA = [A.alloc([T], BF16) for _ in range(2)]; BQA = P.bufs(2, "QA", dma=True)
    KO = [A.alloc([T], BF16) for _ in range(2)]; BKO = P.bufs(2, "KO", dma=True)
    for i in range(2):
        P.op("dve", lambda e, i=i: e.memset(KO[i][64:128, :], 0.0), writes=[BKO[i]])
    VO = [A.alloc([16, 66], BF16) for _ in range(2)]; BVO = P.bufs(2, "VO", dma=True)
    for i in range(2):
        P.op("dve", lambda e, i=i: e.memset(VO[i][:, :, 64:66], 1.0), writes=[BVO[i]])
    cb = A.alloc([2, 256], BF16); Bcb = Buf("cb", P.dsem())
    P.dma("pool", cb.rearrange("p a b -> p (a b)"), g["cb_d"], writes=[Bcb])
    PT = [A.alloc([512], BF16) for _ in range(4)]; BPT = P.bufs(4, "PT")
    rec = A.alloc([512], F32); Brec = Buf("rec"); drec = P.dsem()
    recbc = A.alloc([512], F32); Brecbc = Buf("recbc", P.dsem())
    tmpo = A.alloc([512], F32); Btmpo = Buf("tmpo")
    atth = [A.alloc([T], BF16) for _ in range(2)]; Batth = P.bufs(2, "atth")
    onesf = A.alloc([64], F32); Bonesf = Buf('onesf')
    P.op('dve', lambda e: e.memset(onesf, 1.0), writes=[Bonesf])
    Brecd = Buf("recd")
    agk_o = g["agk_out"].rearrange("(r x) t -> x r t", r=8)
    agv_o = g["agv_out"].rearrange("(r i c p) f -> p i r c f", r=8, i=8, c=2, p=128)
    agv_l = g["agv_in"].rearrange("(n p) f -> p n f", p=128)
    VAv = VA.rearrange("p (i r c) f -> p i r c f", i=8, r=8, c=2)
    KTv = KT.rearrange("p (i r t) -> p i r t", i=8, r=8)
    ntile = 0
    NHEADS = int(os.environ.get('ATT_HEADS', NH))

    def emit_loads_small(h):
        hi = (h + int(os.environ.get('ATT_HI', 0))) % 2
        P.dma("sp", QA[hi][0:64, :], g["q_d"][h * 64:(h + 1) * 64, :], reads=[Bqd], writes=[BQA[hi]])
        P.dma("sp", QA[hi][64:128, :], g["mb_d"][h * 64:(h + 1) * 64, :], reads=[Bmbd], writes=[BQA[hi]])
        P.dma("sp", KO[hi][0:64, :], g["agk_in"][h * 64:(h + 1) * 64, :], reads=[Bagk], writes=[BKO[hi]])
        P.dma("sp", VO[hi][:, :, 0:64], agv_l[:, :, h * 64:(h + 1) * 64], reads=[Bagv], writes=[BVO[hi]])

    def emit_loads_oct(h, i):
        P.dma("sp", KTv[0:64, i, :, :], agk_o[h * 64:(h + 1) * 64, :, i * 256:(i + 1) * 256], reads=[Bagko, Bsel], writes=[BKT[i]])
        for c2 in range(2):
            P.dma("sp", VAv[:, i, :, c2, 0:64], agv_o[:, i, :, c2, h * 64:(h + 1) * 64], reads=[Bagvo, Bva1], writes=[BVA[i]])

    STEP = int(os.environ.get('ATT_STEP', 9))
    emit_loads_small(0)
    for i in range(8):
        emit_loads_oct(0, i)
    for h in range(NHEADS):
        hi = (h + int(os.environ.get('ATT_HI', 0))) % 2
        if h + 1 < NHEADS:
            emit_loads_small(h + 1)
        for b in range(4 if STEP >= 1 else 0):
            P.op("dve", lambda e, b=b: e.memset(pbank[b][:], 0.0), writes=[PB[b]])
        tiles = []
        for k in range(8):
            for kc in range(2):
                tiles.append(("own", k, kc, k * 256, (k + 1) * 256))
        for j in range(int(os.environ.get('ATT_J', 63))):
            kmin = (j + 1) // 8
            r, i = block_loc(j)
            cp = 8 * i + r
            for kc in range(2):
                for b in range(kmin // 2, 4):
                    lo = max(kmin, 2 * b)
                    tiles.append(("past", cp, kc, lo * 256, (2 * b + 2) * 256, i))
        if STEP < 2:
            tiles = []
        LAG = int(os.environ.get('ATT_LAG', 2))
        pend = []

        last_of = {}
        for ti, tl in enumerate(tiles):
            if tl[0] == "past":
                last_of[tl[5]] = ti
        last_at = {ti: i for i, ti in last_of.items()}

        def emit_pv(vl, pt, ab, o0, n_, rds, ti):
            P.op("pe", lambda e: e.matmul(pbank[ab][0:65, o0:o0 + n_], lhsT=vl, rhs=PT[pt][:, 0:n_], start=False, stop=True, skip_group_check=True),
                 reads=rds + [BPT[pt]], writes=[PB[ab]])
            if ti in last_at and h + 1 < NHEADS:
                emit_loads_oct(h + 1, last_at[ti])

        for ti, tl in enumerate(tiles):
            sb = 4 + ntile % 3
            pt = ntile % 4
            ntile += 1
            if tl[0] == "own":
                _, k, kc, c0, c1 = tl
                n_ = c1 - c0
                P.op("pe", lambda e, k=k, kc=kc, c0=c0, c1=c1, sb=sb, n_=n_: e.matmul(pbank[sb][:, 0:n_], lhsT=KO[hi][:, k * 256 + kc * 128:k * 256 + (kc + 1) * 128], rhs=QA[hi][:, c0:c1], start=True, stop=False),
                     reads=[BKO[hi], BQA[hi]], writes=[PB[sb]], signal=False)
                P.op("pe", lambda e, kc=kc, sb=sb, n_=n_: e.matmul(pbank[sb][:, 0:n_], lhsT=identb, rhs=cb[:, kc, :], start=False, stop=True),
                     reads=[Bident, Bcb], writes=[PB[sb]])
                vl = VO[hi][:, k * 2 + kc, 0:65]
                rds = [BVO[hi]]
            else:
                _, cp, kc, c0, c1, i = tl
                n_ = c1 - c0
                P.op("pe", lambda e, cp=cp, kc=kc, c0=c0, c1=c1, sb=sb, n_=n_: e.matmul(pbank[sb][:, 0:n_], lhsT=KT[:, cp * 256 + kc * 128:cp * 256 + (kc + 1) * 128], rhs=QA[hi][:, c0:c1], start=True, stop=True),
                     reads=[BKT[i], Bsel, BQA[hi]], writes=[PB[sb]])
                vl = VA[:, cp * 2 + kc, 0:65]
                rds = [BVA[i], Bva1]
            if os.environ.get('EXP_SKIP', '0') == '0':
              P.op("act", lambda e, sb=sb, pt=pt, n_=n_: e.activation(out=PT[pt][:, 0:n_], in_=pbank[sb][:, 0:n_], func=getattr(AF, os.environ.get('EXP_FUNC', 'Exp')), scale=float(os.environ.get('EXP_SCALE', 1.0))), reads=[PB[sb]], writes=[BPT[pt]])
            ab = c0 // 512
            o0 = c0 % 512
            if STEP < 3:
                continue
            pend.append((vl, pt, ab, o0, n_, rds, ti))
            if len(pend) > LAG:
                emit_pv(*pend.pop(0))
        while pend:
            emit_pv(*pend.pop(0))
        for b in range(int(os.environ.get('ATT_NORM', 4))):
            P.op("dve", lambda e, b=b: e.reciprocal(out=rec[64:65, :], in_=pbank[b][64:65, :]), reads=[PB[b]], writes=[Brec])
            P.op("dve", lambda e, b=b: e.tensor_copy(out=tmpo[0:64, :], in_=pbank[b][0:64, :]), reads=[PB[b]], writes=[Btmpo])
            P.op("pe", lambda e: e.matmul(pbank[7][0:64, :], lhsT=onesf[64:65, :], rhs=rec[64:65, :], start=True, stop=True), reads=[Bonesf, Brec], writes=[PB[7]])
            P.op("dve", lambda e, b=b: e.tensor_tensor(out=atth[hi][0:64, b * 512:(b + 1) * 512], in0=tmpo[0:64, :], in1=pbank[7][0:64, :], op=ALU.mult), reads=[Btmpo, PB[7]], writes=[Batth[hi]])
        P.dma("sp", attT[hi * 64:(hi + 1) * 64, h // 2, :], atth[hi][0:64, :], reads=[Batth[hi]], writes=[BattT[h // 2]])
    P.barrier()

    if UPTO == "C":
        return
    A.off = reg_off
    g['warm_act'](AF.Sigmoid)
    wst = [A.alloc([8, 128], BF16) for _ in range(3)]
    wao = A.alloc([8, 1024], BF16); Bwao = Buf("wao", P.dsem())
    wo = A.alloc([8, 1024], BF16); Bwo = Buf("wo", P.dsem())
    P.dma("pool", wao, g["wao_d"].rearrange("(c p) f -> p c f", p=128), writes=[Bwao])
    P.dma("pool", wo, g["wo_d"].rearrange("(c p) f -> p c f", p=128), writes=[Bwo])
    sga = [A.alloc([512], F32) for _ in range(2)]; Bsga = P.bufs(2, "sga")
    tmpa = [A.alloc([512], F32) for _ in range(2)]; Btmpa = P.bufs(2, "tmpa")
    L = g["alloc_ln"]()
    g["load_ln_params"](L, 1)
    nsg = 0
    for G in range(4):
        for dc in range(8):
            for c in range(8):
                P.op("pe", lambda e, c=c, dc=dc: e.matmul(pbank[0][:], lhsT=wao[:, c, dc * 128:(dc + 1) * 128], rhs=attT[:, c, G * 512:(G + 1) * 512], start=(c == 0), stop=(c == 7)),
                     reads=[Bwao, BattT[c]], writes=[PB[0]], signal=(c == 7))
            w = load_w(6144 + dc * 128)
            fm_mm(w, 1, G)
            si = nsg % 2
            nsg += 1
            P.op("act", lambda e, si=si: e.activation(out=sga[si], in_=pbank[1][:], func=AF.Sigmoid), reads=[PB[1]], writes=[Bsga[si]])
            P.op("dve", lambda e, si=si: e.tensor_tensor(out=tmpa[si], in0=pbank[0][:], in1=sga[si], op=ALU.mult), reads=[PB[0], Bsga[si]], writes=[Btmpa[si]])
            P.op("dve", lambda e, si=si, dc=dc: e.tensor_tensor(out=AT[:, dc, G * 512:(G + 1) * 512], in0=AT[:, dc, G * 512:(G + 1) * 512], in1=tmpa[si], op=ALU.add), reads=[Btmpa[si], BAT[G]], writes=[BAT[G]])
    for t in range(NT):
        yb = [2 + 3 * (t % 2), 3 + 3 * (t % 2)]
        for h in range(2):
            for c in range(8):
                P.op("pe", lambda e, c=c, h=h, t=t, bank=yb[h]: e.matmul(pbank[bank][:], lhsT=AT[:, c, t * 128:(t + 1) * 128], rhs=wo[:, c, h * 512:(h + 1) * 512], start=(c == 0), stop=(c == 7)),
                     reads=[BAT[t // 4], Bwo], writes=[PB[yb[h]]], signal=(c == 7))
        g["ln_tile"](L, t, yb, g["x1_d"], g["x2_d"] if g["stage"] > 2 else g["out_d"], 4 + 3 * (t % 2), final=(g["stage"] == 2))
    P.barrier()


def host_consts(c):
    blks = blocks_of(c)
    pos = np.concatenate([np.arange(b * 256, (b + 1) * 256) for b in blks]).astype(np.float64)
    inv = 500000.0 ** (-np.arange(0, 16, 2, dtype=np.float32) / 16.0)
    ang = pos.astype(np.float32)[:, None] * inv[None, :].astype(np.float32)
    cos, sin = np.cos(ang).T, np.sin(ang).T
    cosT = np.ones((128, T), np.float32)
    sinT = np.zeros((128, T), np.float32)
    for hh in range(2):
        cosT[hh * 64:hh * 64 + 8] = cos
        cosT[hh * 64 + 8:hh * 64 + 16] = cos
        sinT[hh * 64:hh * 64 + 8] = -sin
        sinT[hh * 64 + 8:hh * 64 + 16] = sin
    biasA = np.zeros((8, 64), np.float32)
    biasB = np.zeros((8, 64), np.float32)
    for k in range(8):
        own = blks[k]
        for j in range(64):
            r, i = block_loc(j)
            gp = r * 8 + i
            biasA[k, gp] = 0.0 if j < own else NEG
            biasB[k, gp] = -BIG if j < own else -2 * BIG
    biasA = np.broadcast_to(biasA.reshape(1, 512), (128, 512)).copy()
    biasB = np.broadcast_to(biasB.reshape(1, 512), (128, 512)).copy()
    return dict(cosT=cosT, sinT=sinT, biasA=biasA, biasB=biasB)


def shared_consts():
    bf = ml_dtypes.bfloat16
    pmat = np.zeros((128, 128), np.float32)
    for hh in range(2):
        for m in range(16):
            partner = m + 8 if m < 8 else m - 8
            pmat[hh * 64 + partner, hh * 64 + m] = 1.0
    s_idx = np.arange(128)
    trimask = (s_idx[:, None] <= s_idx[None, :]).astype(np.float32)
    cb = np.zeros((128, 2, 256), np.float32)
    for kc in range(2):
        key = kc * 128 + np.arange(128)
        cb[:, kc, :] = np.where(key[:, None] <= np.arange(256)[None, :], 0.0, -BIG)
    sel = np.zeros((64, S), np.float32)
    for j in range(64):
        r, i = block_loc(j)
        cp = 8 * i + r
        sel[r * 8 + i, cp * 256:(cp + 1) * 256] = 1.0
    return dict(pmat=pmat, trimask=trimask, cb=np.ascontiguousarray(cb.reshape(128, 512)), sel=sel,
                identb=np.eye(128, dtype=np.float32), identf=np.eye(128, dtype=np.float32))


_NC_CACHE = {}


def kernel(x, ffn1_w_gate, ffn1_w_up, ffn1_w_down, ln1_g, ln1_b, w_in,
           gm_ln_g, gm_ln_b, gm_w_s, gm_b_s, w_gm_out, w_att_out, w_o,
           ln2_g, ln2_b, ffn2_w_gate, ffn2_w_up, ffn2_w_down, ln3_g, ln3_b, _stage=3):
    f = lambda a: np.ascontiguousarray(np.asarray(a, dtype=np.float32))
    x = f(x)[0]
    shared = dict(
        wg1=f(ffn1_w_gate)[0], wu1=f(ffn1_w_up)[0], wd1=f(ffn1_w_down)[0], ln1g=f(ln1_g)[0], ln1b=f(ln1_b)[0],
        win=f(w_in)[0], gmg=f(gm_ln_g)[0], gmb=f(gm_ln_b)[0], gmws=f(gm_w_s)[0], gmbs=f(gm_b_s)[0].reshape(1024),
        wgmo=f(w_gm_out)[0], wao=f(w_att_out)[0], wo=f(w_o)[0], ln2g=f(ln2_g)[0], ln2b=f(ln2_b)[0],
        wg2=f(ffn2_w_gate)[0], wu2=f(ffn2_w_up)[0], wd2=f(ffn2_w_down)[0], ln3g=f(ln3_g)[0], ln3b=f(ln3_b)[0])
    shared.update(shared_consts())
    in_maps = []
    for c in range(NCORE):
        blks = blocks_of(c)
        xc = np.concatenate([x[b * 256:(b + 1) * 256] for b in blks], axis=0)
        m = dict(shared)
        m["x"] = np.ascontiguousarray(xc)
        m.update(host_consts(c))
        in_maps.append(m)
    if _stage not in _NC_CACHE:
        _NC_CACHE[_stage] = build_program(_stage)
    nc = _NC_CACHE[_stage]
    res = run_bass_kernel_spmd(nc, in_maps, core_ids=list(range(NCORE)))
    out = np.empty((1, S, D), np.float32)
    for c in range(NCORE):
        oc = res.results[c]["out"]
        for k, b in enumerate(blocks_of(c)):
            out[0, b * 256:(b + 1) * 256] = oc[k * 256:(k + 1) * 256]
    return out
```

```python
import numpy as np
import ml_dtypes
from contextlib import ExitStack
import concourse.bass as bass
import concourse.mybir as mybir
from concourse.bass_utils import run_bass_kernel_spmd

F32 = mybir.dt.float32
BF16 = mybir.dt.bfloat16
AF = mybir.ActivationFunctionType
ALU = mybir.AluOpType
AX = mybir.AxisListType
ENG = ["pe", "act", "dve", "pool", "sp"]

NCORE = 8
S = 16384
T = 2048
NT = 16
D = 1024
DFF = 2816
NFC = 22
NH = 16
ALPHA = float(2.0 ** 0.25)
EPS = 1e-5
BIG = 30000.0
NEG = -1.0e30
ARENA = 207872


class Buf:
    __slots__ = ("w", "r", "name", "dsem", "excl")

    def __init__(self, name="", dsem=None, excl=False):
        self.w = None
        self.r = {}
        self.name = name
        self.dsem = dsem
        self.excl = excl


class DSem:
    __slots__ = ("sem", "count", "key")

    def __init__(self, sem, key):
        self.sem = sem
        self.count = 0
        self.key = key


class _Rec:
    def __init__(self):
        self.calls = []

    def __getattr__(self, name):
        def f(*a, **k):
            self.calls.append((name, a, k))
            return self
        return f


def _bind(fn):
    r = _Rec()
    fn(r)
    assert len(r.calls) == 1, r.calls
    name, a, k = r.calls[0]
    return lambda e: getattr(e, name)(*a, **k)


class Prog:
    def __init__(self, nc, stack):
        self.nc = nc
        self.stack = stack
        self.q = {e: [] for e in ENG}
        self.cnt = {e: 0 for e in ENG}
        self.pending = {e: False for e in ENG}
        self.known = {e: {} for e in ENG}
        self.sems = {e: stack.enter_context(nc.semaphore("s_" + e)) for e in ENG}
        self.semh = dict(self.sems)
        self.dsems = []
        self.trace = {e: [] for e in ENG}
        self.free = []
        self.inuse = []

    def dsem(self, persist=False):
        if not persist and self.free:
            d = self.free.pop()
            self.inuse.append(d)
            return d
        key = ("d", len(self.dsems))
        s = self.stack.enter_context(self.nc.semaphore("d%d" % len(self.dsems)))
        d = DSem(s, key)
        self.semh[key] = s
        self.dsems.append(d)
        if not persist:
            self.inuse.append(d)
        return d

    def bufs(self, n, name="", dma=False, excl=False, persist=False):
        return [Buf("%s%d" % (name, i), self.dsem(persist) if dma else None, excl) for i in range(n)]

    def _wait(self, eng, tok):
        if tok is None:
            return
        k, v = tok
        if self.known[eng].get(k, 0) >= v:
            return
        self.known[eng][k] = v
        self.trace[eng].append(('wait', k, v))
        sem = self.semh[k]
        self.q[eng].append(lambda e, sem=sem, v=v: e.wait_ge(sem, v))

    def _deps(self, eng, reads, writes):
        toks = []
        for b in reads:
            if b.w is not None:
                toks.append((b.w, "raw"))
        for b in writes:
            if b.w is not None:
                toks.append((b.w, "waw"))
            for k, v in b.r.items():
                toks.append(((k, v), "war"))
        for tok, kind in toks:
            k, v = tok
            if k == eng:
                if eng == "pe":
                    continue
                if kind == "war":
                    continue
                if v > self.cnt[eng]:
                    continue
            self._wait(eng, tok)

    def _commit(self, tok, reads, writes):
        for b in writes:
            b.w = tok
            b.r = {}
        for b in reads:
            if b in writes:
                continue
            k, v = tok
            if b.r.get(k, 0) < v:
                b.r[k] = v

    def op(self, eng, fn, reads=(), writes=(), signal=True):
        if any(b.excl for b in reads):
            writes = list(writes) + [b for b in reads if b.excl and b not in writes]
            reads = [b for b in reads if not b.excl]
        self._deps(eng, reads, writes)
        fn = _bind(fn)
        if signal:
            self.cnt[eng] += 1
            sem = self.sems[eng]
            self.q[eng].append(lambda e, fn=fn, sem=sem: fn(e).then_inc(sem, 1))
            self.pending[eng] = False
            tok = (eng, self.cnt[eng])
        else:
            self.q[eng].append(lambda e, fn=fn: fn(e))
            self.pending[eng] = True
            tok = (eng, self.cnt[eng] + 1)
        self.trace[eng].append(('op', tok, signal, [b.name for b in reads], [b.name for b in writes]))
        self._commit(tok, reads, writes)
        return tok

    def dma(self, eng, out_ap, in_ap, reads=(), writes=(), ds=None, inc=16, fn=None):
        if ds is None:
            for b in writes:
                if b.dsem is not None:
                    ds = b.dsem
                    break
        assert ds is not None
        self._deps(eng, reads, writes)
        ds.count += inc
        sem = ds.sem
        if fn is None:
            self.q[eng].append(
                lambda e, o=out_ap, i=in_ap, sem=sem, inc=inc: e.dma_start(out=o, in_=i).then_inc(sem, inc))
        else:
            fn = _bind(fn)
            self.q[eng].append(lambda e, fn=fn, sem=sem, inc=inc: fn(e).then_inc(sem, inc))
        tok = (ds.key, ds.count)
        self.trace[eng].append(('dma', tok, [b.name for b in reads], [b.name for b in writes]))
        self._commit(tok, reads, writes)
        return tok

    def barrier(self):
        for e in ENG:
            assert not self.pending[e], e
        for d in self.dsems:
            if d.count:
                self._wait("sp", (d.key, d.count))
        for o in ENG:
            if o != "sp" and self.cnt[o]:
                self._wait("sp", (o, self.cnt[o]))
        self.cnt["sp"] += 1
        sem = self.sems["sp"]
        self.q["sp"].append(lambda e, sem=sem: e.nop().then_inc(sem, 1))
        tok = ("sp", self.cnt["sp"])
        for e in ENG:
            if e != "sp":
                self._wait(e, tok)
            for o in ENG:
                self.known[e][o] = max(self.known[e].get(o, 0), self.cnt[o])
            for d in self.dsems:
                self.known[e][d.key] = max(self.known[e].get(d.key, 0), d.count)
        self.free.extend(self.inuse)
        self.inuse = []

    def final_wait(self, toks):
        for t in toks:
            self._wait("sp", t)

    def emit(self):
        nc = self.nc
        q = self.q
        with nc.Block() as block:
            @block.tensor
            def _(e):
                for f in q["pe"]:
                    f(e)

            @block.scalar
            def _(e):
                for f in q["act"]:
                    f(e)

            @block.vector
            def _(e):
                for f in q["dve"]:
                    f(e)

            @block.gpsimd
            def _(e):
                for f in q["pool"]:
                    f(e)

            @block.sync
            def _(e):
                for f in q["sp"]:
                    f(e)


class Arena:
    def __init__(self, t):
        self.t = t
        self.off = 0

    def alloc(self, shape, dt):
        n = int(np.prod(shape))
        nb = n * (4 if dt == F32 else 2)
        nb = (nb + 63) // 64 * 64
        assert self.off + nb <= ARENA, (self.off, nb)
        ap = self.t[:, self.off // 2:(self.off + nb) // 2]
        self.off += nb
        if dt == F32:
            ap = ap.bitcast(F32)
        ap = ap[:, 0:n]
        if len(shape) == 2:
            ap = ap.rearrange("p (a b) -> p a b", a=shape[0])
        elif len(shape) == 3:
            ap = ap.rearrange("p (a b c) -> p a b c", a=shape[0], b=shape[1])
        return ap


def blocks_of(c):
    out = []
    for m in range(4):
        out.append(16 * m + c)
        out.append(16 * m + 15 - c)
    return out


def block_loc(j):
    m, t = divmod(j, 16)
    if t < 8:
        return t, 2 * m
    return 15 - t, 2 * m + 1


def build_program(stage=3):
    nc = bass.Bass("TRN2", target_bir_lowering=False)

    def din(name, shape, dt=F32):
        return nc.dram_tensor(name, shape, dt, kind="ExternalInput").ap()

    def dint(name, shape, dt):
        return nc.dram_tensor(name, shape, dt).ap()

    x_d = din("x", [T, D])
    wg_d = [din("wg1", [D, DFF]), din("wg2", [D, DFF])]
    wu_d = [din("wu1", [D, DFF]), din("wu2", [D, DFF])]
    wd_d = [din("wd1", [DFF, D]), din("wd2", [DFF, D])]
    lng_d = [din("ln1g", [D]), din("ln2g", [D]), din("ln3g", [D])]
    lnb_d = [din("ln1b", [D]), din("ln2b", [D]), din("ln3b", [D])]
    win_d = din("win", [D, 7168])
    gmg_d, gmb_d = din("gmg", [D]), din("gmb", [D])
    gmws_d = din("gmws", [8, 128, 128])
    gmbs_d = din("gmbs", [1024])
    wgmo_d, wao_d, wo_d = din("wgmo", [D, D]), din("wao", [D, D]), din("wo", [D, D])
    cos_d, sin_d = din("cosT", [128, T]), din("sinT", [128, T])
    pmat_d = din("pmat", [128, 128])
    tri_d = din("trimask", [128, 128])
    cb_d = din("cb", [128, 512])
    sel_d = din("sel", [64, S])
    biasA_d, biasB_d = din("biasA", [128, 512]), din("biasB", [128, 512])
    identb_d = din("identb", [128, 128])
    identf_d = din("identf", [128, 128])
    out_d = nc.dram_tensor("out", [T, D], F32, kind="ExternalOutput").ap()

    x1_d = dint("x1_d", [T, D], F32)
    x2_d = dint("x2_d", [T, D], F32)
    q_d = dint("q_d", [1024, T], BF16)
    mb_d = dint("mb_d", [1024, T], BF16)
    agk_in_f = dint("agk_in", [1024, T // 2], F32)
    agv_in_f = dint("agv_in", [T, 512], F32)
    agk_out_f = dint("agk_out", [8 * 1024, T // 2], F32)
    agv_out_f = dint("agv_out", [8 * T, 512], F32)
    agk_in, agv_in, agk_out, agv_out = (a.bitcast(BF16) for a in (agk_in_f, agv_in_f, agk_out_f, agv_out_f))
    print("bitcast shapes", agk_in.shape, agv_out.shape)
    km_in = dint("km_in", [128, 64], F32)
    km_out = dint("km_out", [8 * 128, 64], F32)

    with ExitStack() as st:
        P = Prog(nc, st)
        arena_t = st.enter_context(nc.sbuf_tensor("arena", [128, ARENA // 2], BF16))
        pbank = [st.enter_context(nc.psum_tensor("pb%d" % i, [128, 512], F32)) for i in range(8)]
        PB = P.bufs(8, "pb", excl=True)
        A = Arena(arena_t)

        def pbf(i):
            return pbank[i][:].bitcast(BF16)

        xT = A.alloc([8, T], BF16)
        XT = P.bufs(2, "xT")
        identb = A.alloc([128], BF16)
        Bident = Buf("ident", P.dsem(True))
        P.dma("pool", identb, identb_d, writes=[Bident])
        warm_t = A.alloc([64], F32)
        Bwarm = Buf('warm')
        base_off = A.off

        out_toks = []

        def transpose_into_xT(xb_ap, Bxb, t, bank):
            pv = pbf(bank)
            for c in range(8):
                P.op("pe", lambda e, c=c: e.transpose(out=pv[:, c * 128:(c + 1) * 128], in_=xb_ap[:, c * 128:(c + 1) * 128], identity=identb),
                     reads=[Bxb, Bident], writes=[PB[bank]], signal=(c == 7))
            P.op("dve", lambda e: e.tensor_copy(out=xT[:, :, t * 128:(t + 1) * 128], in_=pv.rearrange("p (a b) -> p a b", a=8)),
                 reads=[PB[bank]], writes=[XT[t // 8]])

        def rstd_newton(st, Bst, var_c, out_c, tmp_c):
            vh = st[:, tmp_c:tmp_c + 1]
            t = st[:, tmp_c + 1:tmp_c + 2]
            y = st[:, out_c:out_c + 1]
            P.op("dve", lambda e: e.tensor_scalar(out=vh, in0=st[:, var_c:var_c + 1], scalar1=EPS, scalar2=-0.5, op0=ALU.add, op1=ALU.mult), reads=[Bst], writes=[Bst])
            P.op("dve", lambda e: e.tensor_scalar(out=t, in0=st[:, var_c:var_c + 1], scalar1=0.5, scalar2=0.5 + 0.5 * EPS, op0=ALU.mult, op1=ALU.add), reads=[Bst], writes=[Bst])
            P.op("dve", lambda e: e.reciprocal(out=y, in_=t), reads=[Bst], writes=[Bst])
            for _ in range(8):
                P.op("dve", lambda e: e.scalar_tensor_tensor(out=t, in0=y, scalar=vh, in1=y, op0=ALU.mult, op1=ALU.mult), reads=[Bst], writes=[Bst])
                P.op("dve", lambda e: e.scalar_tensor_tensor(out=y, in0=t, scalar=1.5, in1=y, op0=ALU.add, op1=ALU.mult), reads=[Bst], writes=[Bst])

        def warm_act(func):
            P.barrier()
            P.op("dve", lambda e: e.memset(warm_t, 0.0), writes=[Bwarm])
            P.op("act", lambda e: e.activation(out=warm_t, in_=warm_t, func=func), reads=[Bwarm], writes=[Bwarm])
            P.barrier()

        class LNBufs:
            pass

        def alloc_ln():
            L = LNBufs()
            L.xtm = [A.alloc([D], F32) for _ in range(2)]
            L.Bxtm = P.bufs(2, "xtm", dma=True)
            L.z = A.alloc([D], F32)
            L.Bz = Buf("z")
            L.xo = [A.alloc([D], F32) for _ in range(2)]
            L.Bxo = P.bufs(2, "xo")
            L.xb = [A.alloc([D], BF16) for _ in range(2)]
            L.Bxb = P.bufs(2, "xb")
            L.gbc = A.alloc([D], F32)
            L.bbc = A.alloc([D], F32)
            L.Bgb = Buf("gb", P.dsem())
            L.st = A.alloc([20], F32)
            L.Bst = Buf("st")
            L.dst = [P.dsem(True) for _ in range(2)]
            L.n = 0
            return L

        def load_ln_params(L, li):
            P.dma("sp", L.gbc, lng_d[li].partition_broadcast(128), writes=[L.Bgb])
            P.dma("sp", L.bbc, lnb_d[li].partition_broadcast(128), writes=[L.Bgb])

        def ln_tile(L, t, ybanks, res_d, dst_d, tr_bank, final=False):
            i = L.n % 2
            L.n += 1
            P.dma("sp", L.xtm[i], res_d[t * 128:(t + 1) * 128, :], writes=[L.Bxtm[i]])
            for h in range(2):
                P.op("dve", lambda e, h=h: e.scalar_tensor_tensor(out=L.z[:, h * 512:(h + 1) * 512], in0=L.xtm[i][:, h * 512:(h + 1) * 512], scalar=ALPHA,
                                                                in1=pbank[ybanks[h]][:], op0=ALU.mult, op1=ALU.add),
                     reads=[L.Bxtm[i], PB[ybanks[h]]], writes=[L.Bz])
            for h in range(2):
                P.op("dve", lambda e, h=h: e.bn_stats(out=L.st[:, h * 6:(h + 1) * 6], in_=L.z[:, h * 512:(h + 1) * 512]), reads=[L.Bz], writes=[L.Bst])
            P.op("dve", lambda e: e.bn_aggr(out=L.st[:, 12:14], in_=L.st[:, 0:12]), reads=[L.Bst], writes=[L.Bst])
            rstd_newton(L.st, L.Bst, 13, 15, 16)
            P.op("dve", lambda e: e.tensor_scalar(out=L.z[:], in0=L.z[:], scalar1=L.st[:, 12:13], scalar2=L.st[:, 15:16], op0=ALU.subtract, op1=ALU.mult),
                 reads=[L.Bz, L.Bst], writes=[L.Bz])
            P.op("dve", lambda e: e.tensor_tensor(out=L.z[:], in0=L.z[:], in1=L.gbc, op=ALU.mult), reads=[L.Bz, L.Bgb], writes=[L.Bz])
            P.op("dve", lambda e: e.tensor_tensor(out=L.xo[i], in0=L.z[:], in1=L.bbc, op=ALU.add), reads=[L.Bz, L.Bgb], writes=[L.Bxo[i]])
            tok = P.dma("sp", dst_d[t * 128:(t + 1) * 128, :], L.xo[i], reads=[L.Bxo[i]], ds=L.dst[i])
            if final:
                out_toks.append(tok)
            else:
                P.op("act", lambda e: e.copy(out=L.xb[i], in_=L.xo[i]), reads=[L.Bxo[i]], writes=[L.Bxb[i]])
                transpose_into_xT(L.xb[i], L.Bxb[i], t, tr_bank)

        def ffn_phase(fi, li, res_d, dst_d, first, final):
            A.off = base_off
            warm_act(AF.Silu)
            hT = A.alloc([NFC, 1024], BF16)
            HT = P.bufs(NFC, "hT")
            wd = A.alloc([NFC, D], BF16)
            WD = P.bufs(NFC, "wd")
            dwd = P.dsem()
            for b_ in WD:
                b_.dsem = dwd
            wgu = [[A.alloc([8, 128], BF16) for _ in range(2)] for _ in range(3)]
            WGU = [P.bufs(2, "wgu", dma=True) for _ in range(3)]
            sg = [A.alloc([512], F32) for _ in range(4)]
            SG = P.bufs(4, "sg")
            L = alloc_ln()
            load_ln_params(L, li)
            if first:
                for t in range(NT):
                    i = t % 2
                    P.dma("sp", L.xtm[i], x_d[t * 128:(t + 1) * 128, :], writes=[L.Bxtm[i]])
                    P.op("act", lambda e, i=i: e.copy(out=L.xb[i], in_=L.xtm[i]), reads=[L.Bxtm[i]], writes=[L.Bxb[i]])
                    transpose_into_xT(L.xb[i], L.Bxb[i], t, t % 4)
            wgr = wg_d[fi].rearrange("(c p) f -> p c f", p=128)
            wur = wu_d[fi].rearrange("(c p) f -> p c f", p=128)
            for fc in range(NFC):
                P.dma("pool", wd[:, fc, :], wd_d[fi][fc * 128:(fc + 1) * 128, :], writes=[WD[fc]])
            nw = 0
            nsg = 0
            for stt in range(2):
                for fc in range(NFC):
                    w = nw % 3
                    nw += 1
                    P.dma("pool", wgu[w][0], wgr[:, :, fc * 128:(fc + 1) * 128], writes=[WGU[w][0]])
                    P.dma("pool", wgu[w][1], wur[:, :, fc * 128:(fc + 1) * 128], writes=[WGU[w][1]])
                    bset = 4 * (fc % 2)
                    for h in range(2):
                        cols = slice(stt * 1024 + h * 512, stt * 1024 + (h + 1) * 512)
                        for gu in range(2):
                            bank = bset + 2 * gu + h
                            for c in range(8):
                                P.op("pe", lambda e, bank=bank, c=c, w=w, gu=gu, cols=cols: e.matmul(pbank[bank][:], lhsT=wgu[w][gu][:, c, :], rhs=xT[:, c, cols], start=(c == 0), stop=(c == 7)),
                                     reads=[WGU[w][gu], XT[stt]], writes=[PB[bank]], signal=(c == 7))
                        si = nsg % 4
                        nsg += 1
                        P.op("act", lambda e, si=si, bank=bset + h: e.activation(out=sg[si], in_=pbank[bank][:], func=AF.Silu), reads=[PB[bset + h]], writes=[SG[si]])
                        P.op("dve", lambda e, si=si, bank=bset + 2 + h, fc=fc, h=h: e.scalar_tensor_tensor(out=hT[:, fc, h * 512:(h + 1) * 512], in0=pbank[bank][:], scalar=0.5, in1=sg[si],
                                                                                                   op0=ALU.mult, op1=ALU.mult),
                             reads=[PB[bset + 2 + h], SG[si]], writes=[HT[fc]])
                for tt in range(8):
                    t = stt * 8 + tt
                    yb = [4 * (tt % 2), 4 * (tt % 2) + 1]
                    for h in range(2):
                        for fc in range(NFC):
                            P.op("pe", lambda e, h=h, fc=fc, tt=tt, bank=yb[h]: e.matmul(pbank[bank][:], lhsT=hT[:, fc, tt * 128:(tt + 1) * 128], rhs=wd[:, fc, h * 512:(h + 1) * 512],
                                                                                     start=(fc == 0), stop=(fc == NFC - 1)),
                                 reads=[HT[fc], WD[fc]], writes=[PB[yb[h]]], signal=(fc == NFC - 1))
                    ln_tile(L, t, yb, res_d, dst_d, 4 * (tt % 2) + 2, final=final)
            P.barrier()

        ffn_phase(0, 0, x_d, x1_d if stage > 1 else out_d, True, stage == 1)

        if stage >= 2:
            mixer_phase(nc, P, A, base_off, locals())
        if stage >= 3:
            ffn_phase(1, 2, x2_d, out_d, False, True)

        P.final_wait(out_toks)
        P.emit()
    return nc


def mixer_phase(nc, P, A, base_off, env):
    import os
    g = env
    xT, XT, pbank, PB, pbf, identb, Bident = g["xT"], g["XT"], g["pbank"], g["PB"], g["pbf"], g["identb"], g["Bident"]
    win_d = g["win_d"]
    winr = win_d.rearrange("(c p) f -> p c f", p=128)
    A.off = base_off
    AT = A.alloc([8, T], BF16)
    BAT = P.bufs(4, "AT")
    attT = A.alloc([8, T], BF16)
    BattT = P.bufs(8, "attT", dma=True, persist=True)
    reg_off = A.off
    cosT = A.alloc([T], F32); sinT = A.alloc([T], F32)
    Bcs = Buf("cs", P.dsem())
    P.dma("sp", cosT, g["cos_d"], writes=[Bcs]); P.dma("sp", sinT, g["sin_d"], writes=[Bcs])
    pmat = A.alloc([128], BF16); Bpm = Buf("pm", P.dsem())
    P.dma("pool", pmat, g["pmat_d"], writes=[Bpm])
    wst = [A.alloc([8, 128], BF16) for _ in range(3)]; WST = P.bufs(3, "wst", dma=True)
    wtm = A.alloc([8, 1024], BF16); Bwtm = Buf("wtm", P.dsem())
    kraw = [A.alloc([512], BF16) for _ in range(2)]; Bkraw = P.bufs(2, "kraw")
    t1 = [A.alloc([512], F32) for _ in range(2)]; Bt1 = P.bufs(2, "t1")
    t2 = [A.alloc([512], F32) for _ in range(2)]; Bt2 = P.bufs(2, "t2")
    krot = [A.alloc([512], F32) for _ in range(2)]; Bkrot = P.bufs(2, "krot")
    kbf = [A.alloc([512], BF16) for _ in range(2)]; Bkbf = P.bufs(2, "kbf"); dkbf = [P.dsem() for _ in range(2)]
    vsb = [A.alloc([1024], BF16) for _ in range(2)]; Bvsb = P.bufs(2, "vsb"); dvsb = [P.dsem() for _ in range(2)]
    kst = A.alloc([16], F32); Bkst = Buf("kst")
    kmloc = A.alloc([8, 8], F32); Bkmloc = Buf("kmloc")
    kmall2 = A.alloc([8, 64], F32); Bkm2 = Buf("km2", P.dsem())
    kmall = A.alloc([8, 64], F32); Bkmall = Buf("kmall")
    bA = A.alloc([512], F32); bB = A.alloc([512], F32); BbAB = Buf("bAB", P.dsem())
    P.dma("sp", bA, g["biasA_d"], writes=[BbAB]); P.dma("sp", bB, g["biasB_d"], writes=[BbAB])
    Gms = [A.alloc([64], F32) for _ in range(4)]; BGms = P.bufs(4, "Gm")
    mx8s = [A.alloc([8], F32) for _ in range(4)]; Bmxs = P.bufs(4, "mx")
    gcnt = [0]
    MBq = [A.alloc([128], BF16) for _ in range(2)]; BMBq = P.bufs(2, "MBq")
    mbT = [A.alloc([512], BF16) for _ in range(2)]; BmbT = P.bufs(2, "mbT"); dmbT = [P.dsem() for _ in range(2)]
    for i in range(2):
        P.op("dve", lambda e, i=i: e.memset(MBq[i], 0.0), writes=[BMBq[i]])
    nws = [0]
    Bagk = Buf("agk", P.dsem()); Bagv = Buf("agv", P.dsem()); Bkmin = Buf("kmin", P.dsem())
    Bagko = Buf("agko", P.dsem()); Bagvo = Buf("agvo", P.dsem()); Bkmo = Buf("kmo", P.dsem())
    Bqd = Buf("qd"); Bmbd = Buf("mbd")
    grp = [list(range(NCORE))]

    def proj_fm(col0, bank, tt):
        w = nws[0] % 3
        return w

    def load_w(col0):
        w = nws[0] % 3
        nws[0] += 1
        P.dma("pool", wst[w], winr[:, :, col0:col0 + 128], writes=[WST[w]])
        return w

    def fm_mm(w, bank, tt):
        for c in range(8):
            P.op("pe", lambda e, c=c: e.matmul(pbank[bank][:], lhsT=wst[w][:, c, :], rhs=xT[:, c, tt * 512:(tt + 1) * 512], start=(c == 0), stop=(c == 7)),
                 reads=[WST[w], XT[tt // 2]], writes=[PB[bank]], signal=(c == 7))

    def rope(bank, bank2, tt, n, scale):
        i = n % 2
        P.op("act", lambda e: e.copy(out=kraw[i], in_=pbank[bank][:]), reads=[PB[bank]], writes=[Bkraw[i]])
        P.op("pe", lambda e: e.matmul(pbank[bank2][:], lhsT=pmat, rhs=kraw[i], start=True, stop=True), reads=[Bpm, Bkraw[i]], writes=[PB[bank2]])
        cs = slice(tt * 512, (tt + 1) * 512)
        P.op("dve", lambda e: e.tensor_tensor(out=t1[i], in0=pbank[bank][:], in1=cosT[:, cs], op=ALU.mult), reads=[PB[bank], Bcs], writes=[Bt1[i]])
        P.op("dve", lambda e: e.tensor_tensor(out=t2[i], in0=pbank[bank2][:], in1=sinT[:, cs], op=ALU.mult), reads=[PB[bank2], Bcs], writes=[Bt2[i]])
        P.op("dve", lambda e: e.tensor_tensor(out=krot[i], in0=t1[i], in1=t2[i], op=ALU.add), reads=[Bt1[i], Bt2[i]], writes=[Bkrot[i]])
        P.op("act", lambda e: e.activation(out=kbf[i], in_=krot[i], func=AF.Copy, scale=scale), reads=[Bkrot[i]], writes=[Bkbf[i]])
        return i

    n = 0
    for pr in range(8):
        w = load_w(3072 + pr * 128)
        for tt in range(4):
            bank = (n % 2) * 2
            fm_mm(w, bank, tt)
            i = rope(bank, bank + 1, tt, n, 1.0)
            n += 1
            P.dma("sp", g["agk_in"][pr * 128:(pr + 1) * 128, tt * 512:(tt + 1) * 512], kbf[i], reads=[Bkbf[i]], writes=[Bagk], ds=dkbf[i])
            for hb in range(2):
                P.op("dve", lambda e, hb=hb, i=i: e.bn_stats(out=kst[:, 0:6], in_=krot[i][:, hb * 256:(hb + 1) * 256]), reads=[Bkrot[i]], writes=[Bkst])
                P.op("dve", lambda e, hb=hb, pr=pr, tt=tt: e.bn_aggr(out=kst[:, 8:10], in_=kst[:, 0:6]), reads=[Bkst], writes=[Bkst])
                P.op("dve", lambda e, hb=hb, pr=pr, tt=tt: e.tensor_copy(out=kmloc[:, pr, tt * 2 + hb:tt * 2 + hb + 1], in_=kst[:, 8:9]), reads=[Bkst], writes=[Bkmloc])
    P.dma("sp", g["km_in"], kmloc.rearrange("p a b -> p (a b)"), reads=[Bkmloc], writes=[Bkmin])
    NOCC = os.environ.get("NOCC", "0") == "1"
    if not NOCC:
      P.dma("pool", None, None, reads=[Bkmin], writes=[Bkmo], inc=1,
          fn=lambda e: e.collective_compute("AllGather", ALU.bypass, replica_groups=grp, ins=[g["km_in"]], outs=[g["km_out"]]))
    if not NOCC:
      P.dma("pool", None, None, reads=[Bagk], writes=[Bagko], inc=1,
          fn=lambda e: e.collective_compute("AllGather", ALU.bypass, replica_groups=grp, ins=[g["agk_in_f"]], outs=[g["agk_out_f"]]))
    P.dma("pool", wtm, winr[:, :, 4096:5120], writes=[Bwtm])
    for t in range(NT):
        i = t % 2
        for h in range(2):
            bank = 4 + h
            for c in range(8):
                P.op("pe", lambda e, c=c, h=h, t=t, bank=bank: e.matmul(pbank[bank][:], lhsT=xT[:, c, t * 128:(t + 1) * 128], rhs=wtm[:, c, h * 512:(h + 1) * 512], start=(c == 0), stop=(c == 7)),
                     reads=[Bwtm, XT[t // 8]], writes=[PB[bank]], signal=(c == 7))
            P.op("act", lambda e, h=h, i=i, bank=bank: e.copy(out=vsb[i][:, h * 512:(h + 1) * 512], in_=pbank[bank][:]), reads=[PB[bank]], writes=[Bvsb[i]])
        P.dma("sp", g["agv_in"][t * 128:(t + 1) * 128, :], vsb[i], reads=[Bvsb[i]], writes=[Bagv], ds=dvsb[i])
    if not NOCC:
      P.dma("pool", None, None, reads=[Bagv], writes=[Bagvo], inc=1,
          fn=lambda e: e.collective_compute("AllGather", ALU.bypass, replica_groups=grp, ins=[g["agv_in_f"]], outs=[g["agv_out_f"]]))
    P.dma("sp", kmall2.rearrange("p a b -> p (a b)").rearrange("p (r c) -> p r c", r=8), g["km_out"].rearrange("(r p) c -> p r c", p=128), reads=[Bkmo], writes=[Bkm2])
    for pr in range(8):
        P.op("dve", lambda e, pr=pr: e.tensor_copy(out=kmall[:, pr, :].rearrange("p (r i) -> p r i", r=8), in_=kmall2[:, :, pr * 8:(pr + 1) * 8]), reads=[Bkm2], writes=[Bkmall])
    for pr in range(8):
        w = load_w(2048 + pr * 128)
        for tt in range(4):
            bank = (n % 2) * 2
            fm_mm(w, bank, tt)
            i = rope(bank, bank + 1, tt, n, 0.125)
            n += 1
            P.dma("sp", g["q_d"][pr * 128:(pr + 1) * 128, tt * 512:(tt + 1) * 512], kbf[i], reads=[Bkbf[i]], writes=[Bqd], ds=dkbf[i])
            for par in range(2):
                hb = par * 64
                mi = (n * 2 + par) % 2
                for qt in range(4):
                    slot = (tt * 4 + qt) // 2
                    gc = gcnt[0]
                    gcnt[0] += 1
                    Gm, BGm, mx8, Bmx = Gms[gc % 4], BGms[gc % 4], mx8s[gc % 4], Bmxs[gc % 4]
                    gb = 4 + gc % 2
                    tb = 6 + gc % 2
                    P.op("pe", lambda e, hb=hb, qt=qt, i=i, pr=pr: e.matmul(pbank[gb][:, 0:64], lhsT=krot[i][hb:hb + 64, qt * 128:(qt + 1) * 128], rhs=kmall[hb:hb + 64, pr, :], start=True, stop=True),
                         reads=[Bkrot[i], Bkmall], writes=[PB[gb]])
                    P.op("dve", lambda e, slot=slot: e.tensor_tensor(out=Gm, in0=pbank[gb][:, 0:64], in1=bA[:, slot * 64:(slot + 1) * 64], op=ALU.add), reads=[PB[gb], BbAB], writes=[BGm])
                    P.op("dve", lambda e: e.max(out=mx8, in_=Gm), reads=[BGm], writes=[Bmx])
                    P.op("dve", lambda e: e.tensor_scalar(out=Gm, in0=Gm, scalar1=mx8[:, 2:3], scalar2=BIG, op0=ALU.is_ge, op1=ALU.mult), reads=[BGm, Bmx], writes=[BGm])
                    mq = qt % 2
                    P.op("dve", lambda e, slot=slot, mq=mq: e.tensor_tensor(out=MBq[mq][:, 64:128], in0=Gm, in1=bB[:, slot * 64:(slot + 1) * 64], op=ALU.add), reads=[BGm, BbAB], writes=[BMBq[mq]])
                    pv = pbf(tb)
                    P.op("pe", lambda e, mq=mq, pv=pv: e.transpose(out=pv[:, 0:128], in_=MBq[mq], identity=identb), reads=[BMBq[mq], Bident], writes=[PB[tb]])
                    P.op("act", lambda e, mi=mi, qt=qt, pv=pv: e.copy(out=mbT[mi][64:128, qt * 128:(qt + 1) * 128], in_=pv[64:128, 0:128]), reads=[PB[tb]], writes=[BmbT[mi]])
                h = pr * 2 + par
                P.dma("sp", g["mb_d"][h * 64:(h + 1) * 64, tt * 512:(tt + 1) * 512], mbT[mi][64:128, :], reads=[BmbT[mi]], writes=[Bmbd], ds=dmbT[mi])
    P.barrier()

    import os
    UPTO = os.environ.get("MIX_UPTO", "Z")
    if UPTO == "A":
        return
    A.off = reg_off
    g['warm_act'](AF.Gelu_apprx_tanh)
    wst = [A.alloc([8, 128], BF16) for _ in range(3)]
    wtm = A.alloc([8, 1024], BF16)
    wgmo = A.alloc([8, 1024], BF16); Bwgmo = Buf("wgmo", P.dsem())
    P.dma("pool", wtm, winr[:, :, 1024:2048], writes=[Bwtm])
    P.dma("pool", wgmo, g["wgmo_d"].rearrange("(c p) f -> p c f", p=128), writes=[Bwgmo])
    wsraw = A.alloc([8, 128], BF16); Bwsraw = Buf("wsraw", P.dsem())
    P.dma("pool", wsraw, g["gmws_d"].rearrange("g t s -> t g s"), writes=[Bwsraw])
    tri = A.alloc([128], F32); Btri = Buf("tri", P.dsem())
    P.dma("sp", tri, g["tri_d"], writes=[Btri])
    wsT = A.alloc([8, 128], BF16); BwsT = Buf("wsT")
    for gg in range(8):
        pv = pbf(7)
        P.op("pe", lambda e, gg=gg, pv=pv: e.transpose(out=pv[:, 0:128], in_=wsraw[:, gg, :], identity=identb), reads=[Bwsraw, Bident], writes=[PB[7]])
        P.op("dve", lambda e, gg=gg, pv=pv: e.tensor_tensor(out=wsT[:, gg, :], in0=pv[:, 0:128], in1=tri, op=ALU.mult), reads=[PB[7], Btri], writes=[BwsT])
    bsbc = A.alloc([8, 128], F32); gmgb = A.alloc([D], F32); gmbb = A.alloc([D], F32); Bgmp = Buf("gmp", P.dsem())
    P.dma("sp", bsbc.rearrange("p a b -> p (a b)"), g["gmbs_d"].partition_broadcast(128), writes=[Bgmp])
    P.dma("sp", gmgb, g["gmg_d"].partition_broadcast(128), writes=[Bgmp])
    P.dma("sp", gmbb, g["gmb_d"].partition_broadcast(128), writes=[Bgmp])
    ug = [A.alloc([8, 512], BF16) for _ in range(2)]; Bug = P.bufs(2, "ug")
    vg = A.alloc([D], F32); Bvg = Buf("vg")
    vn = [A.alloc([D], BF16) for _ in range(2)]; Bvn = P.bufs(2, "vn")
    gst = A.alloc([20], F32); Bgst = Buf("gst")
    svs = A.alloc([8, 128], F32); Bsvs = Buf("svs")
    mT = [A.alloc([8, 512], BF16) for _ in range(2)]; BmT = P.bufs(2, "mT")
    sgm = [A.alloc([512], F32) for _ in range(2)]; Bsgm = P.bufs(2, "sgm")
    nsg = 0
    for G in range(4):
        gi = G % 2
        for c8 in range(8):
            w = load_w(c8 * 128)
            bank = c8 % 2
            fm_mm(w, bank, G)
            P.op("act", lambda e, c8=c8, bank=bank: e.activation(out=ug[gi][:, c8, :], in_=pbank[bank][:], func=AF.Gelu_apprx_tanh), reads=[PB[bank]], writes=[Bug[gi]])
        for q4 in range(4):
            t = G * 4 + q4
            vi = t % 2
            for h in range(2):
                bank = 2 + h
                for c in range(8):
                    P.op("pe", lambda e, c=c, h=h, t=t, bank=bank: e.matmul(pbank[bank][:], lhsT=xT[:, c, t * 128:(t + 1) * 128], rhs=wtm[:, c, h * 512:(h + 1) * 512], start=(c == 0), stop=(c == 7)),
                         reads=[Bwtm, XT[t // 8]], writes=[PB[bank]], signal=(c == 7))
                P.op("act", lambda e, h=h, bank=bank: e.activation(out=vg[:, h * 512:(h + 1) * 512], in_=pbank[bank][:], func=AF.Gelu_apprx_tanh), reads=[PB[bank]], writes=[Bvg])
            for h in range(2):
                P.op("dve", lambda e, h=h: e.bn_stats(out=gst[:, h * 6:(h + 1) * 6], in_=vg[:, h * 512:(h + 1) * 512]), reads=[Bvg], writes=[Bgst])
            P.op("dve", lambda e: e.bn_aggr(out=gst[:, 12:14], in_=gst[:, 0:12]), reads=[Bgst], writes=[Bgst])
            g["rstd_newton"](gst, Bgst, 13, 15, 16)
            P.op("dve", lambda e: e.tensor_scalar(out=vg, in0=vg, scalar1=gst[:, 12:13], scalar2=gst[:, 15:16], op0=ALU.subtract, op1=ALU.mult), reads=[Bvg, Bgst], writes=[Bvg])
            P.op("dve", lambda e: e.tensor_tensor(out=vg, in0=vg, in1=gmgb, op=ALU.mult), reads=[Bvg, Bgmp], writes=[Bvg])
            P.op("dve", lambda e, vi=vi: e.tensor_tensor(out=vn[vi], in0=vg, in1=gmbb, op=ALU.add), reads=[Bvg, Bgmp], writes=[Bvn[vi]])
            for hb in range(2):
                bank = 4 + hb
                for g4 in range(4):
                    gg = hb * 4 + g4
                    P.op("pe", lambda e, gg=gg, g4=g4, vi=vi, bank=bank: e.matmul(pbank[bank][:, g4 * 128:(g4 + 1) * 128], lhsT=vn[vi][:, gg * 128:(gg + 1) * 128], rhs=wsT[:, gg, :], start=True, stop=True),
                         reads=[Bvn[vi], BwsT], writes=[PB[bank]], signal=(g4 == 3))
                P.op("dve", lambda e, hb=hb, bank=bank: e.tensor_tensor(out=svs[:, hb * 4:(hb + 1) * 4, :], in0=pbank[bank][:].rearrange("p (a b) -> p a b", a=4), in1=bsbc[:, hb * 4:(hb + 1) * 4, :], op=ALU.add),
                     reads=[PB[bank], Bgmp], writes=[Bsvs])
            P.op("dve", lambda e, q4=q4: e.tensor_tensor(out=mT[gi][:, :, q4 * 128:(q4 + 1) * 128], in0=svs, in1=ug[gi][:, :, q4 * 128:(q4 + 1) * 128], op=ALU.mult),
                 reads=[Bsvs, Bug[gi]], writes=[BmT[gi]])
        for dc in range(8):
            bank = 6
            for c in range(8):
                P.op("pe", lambda e, c=c, dc=dc: e.matmul(pbank[6][:], lhsT=wgmo[:, c, dc * 128:(dc + 1) * 128], rhs=mT[gi][:, c, :], start=(c == 0), stop=(c == 7)),
                     reads=[Bwgmo, BmT[gi]], writes=[PB[6]], signal=(c == 7))
            w = load_w(5120 + dc * 128)
            fm_mm(w, 7, G)
            si = nsg % 2
            nsg += 1
            P.op("act", lambda e, si=si: e.activation(out=sgm[si], in_=pbank[7][:], func=AF.Tanh, scale=0.5), reads=[PB[7]], writes=[Bsgm[si]])
            P.op("dve", lambda e, si=si: e.tensor_scalar(out=sgm[si], in0=sgm[si], scalar1=0.5, scalar2=0.5, op0=ALU.mult, op1=ALU.add), reads=[Bsgm[si]], writes=[Bsgm[si]])
            P.op("dve", lambda e, si=si, dc=dc: e.tensor_tensor(out=AT[:, dc, G * 512:(G + 1) * 512], in0=pbank[6][:], in1=sgm[si], op=ALU.mult), reads=[PB[6], Bsgm[si]], writes=[BAT[G]])
    P.barrier()

    if UPTO == "B":
        return
    print('CNT before attention', P.cnt)
    P.mark = {e: len(P.trace[e]) for e in ENG}
    A.off = reg_off
    if os.environ.get('EXP_WARM', '1') == '1':
        dmy = A.alloc([64], F32); Bdmy = Buf('dmy')
        P.op('dve', lambda e: e.memset(dmy, 0.0), writes=[Bdmy])
        P.op('act', lambda e: e.activation(out=dmy, in_=dmy, func=AF.Exp), reads=[Bdmy], writes=[Bdmy])
        P.barrier()
    KT = A.alloc([S], BF16)
    BKT = P.bufs(8, "KT", dma=True)
    Bsel = Buf("sel", P.dsem())
    P.dma("pool", KT[64:128, :], g["sel_d"], writes=[Bsel])
    VA = A.alloc([128, 66], BF16)
    BVA = P.bufs(8, "VA", dma=True)
    Bva1 = Buf("va1")
    P.op("dve", lambda e: e.memset(VA[:, :, 64:66], 1.0), writes=[Bva1])
    QA = [A.alloc([T], BF16) for _ in range(2)]; BQA = P.bufs(2, "QA", dma=True)
    KO = [A.alloc([T], BF16) for _ in range(2)]; BKO = P.bufs(2, "KO", dma=True)
    for i in range(2):
        P.op("dve", lambda e, i=i: e.memset(KO[i][64:128, :], 0.0), writes=[BKO[i]])
    VO = [A.alloc([16, 66], BF16) for _ in range(2)]; BVO = P.bufs(2, "VO", dma=True)
    for i in range(2):
        P.op("dve", lambda e, i=i: e.memset(VO[i][:, :, 64:66], 1.0), writes=[BVO[i]])
    cb = A.alloc([2, 256], BF16); Bcb = Buf("cb", P.dsem())
    P.dma("pool", cb.rearrange("p a b -> p (a b)"), g["cb_d"], writes=[Bcb])
    PT = [A.alloc([512], BF16) for _ in range(4)]; BPT = P.bufs(4, "PT")
    rec = A.alloc([512], F32); Brec = Buf("rec"); drec = P.dsem()
    recbc = A.alloc([512], F32); Brecbc = Buf("recbc", P.dsem())
    tmpo = A.alloc([512], F32); Btmpo = Buf("tmpo")
    atth = [A.alloc([T], BF16) for _ in range(2)]; Batth = P.bufs(2, "atth")
    onesf = A.alloc([64], F32); Bonesf = Buf('onesf')
    P.op('dve', lambda e: e.memset(onesf, 1.0), writes=[Bonesf])
    Brecd = Buf("recd")
    agk_o = g["agk_out"].rearrange("(r x) t -> x r t", r=8)
    agv_o = g["agv_out"].rearrange("(r i c p) f -> p i r c f", r=8, i=8, c=2, p=128)
    agv_l = g["agv_in"].rearrange("(n p) f -> p n f", p=128)
    VAv = VA.rearrange("p (i r c) f -> p i r c f", i=8, r=8, c=2)
    KTv = KT.rearrange("p (i r t) -> p i r t", i=8, r=8)
    ntile = 0
    NHEADS = int(os.environ.get('ATT_HEADS', NH))

    def emit_loads_small(h):
        hi = (h + int(os.environ.get('ATT_HI', 0))) % 2
        P.dma("sp", QA[hi][0:64, :], g["q_d"][h * 64:(h + 1) * 64, :], reads=[Bqd], writes=[BQA[hi]])
        P.dma("sp", QA[hi][64:128, :], g["mb_d"][h * 64:(h + 1) * 64, :], reads=[Bmbd], writes=[BQA[hi]])
        P.dma("sp", KO[hi][0:64, :], g["agk_in"][h * 64:(h + 1) * 64, :], reads=[Bagk], writes=[BKO[hi]])
        P.dma("sp", VO[hi][:, :, 0:64], agv_l[:, :, h * 64:(h + 1) * 64], reads=[Bagv], writes=[BVO[hi]])

    def emit_loads_oct(h, i):
        P.dma("sp", KTv[0:64, i, :, :], agk_o[h * 64:(h + 1) * 64, :, i * 256:(i + 1) * 256], reads=[Bagko, Bsel], writes=[BKT[i]])
        for c2 in range(2):
            P.dma("sp", VAv[:, i, :, c2, 0:64], agv_o[:, i, :, c2, h * 64:(h + 1) * 64], reads=[Bagvo, Bva1], writes=[BVA[i]])

    STEP = int(os.environ.get('ATT_STEP', 9))
    emit_loads_small(0)
    for i in range(8):
        emit_loads_oct(0, i)
    for h in range(NHEADS):
        hi = (h + int(os.environ.get('ATT_HI', 0))) % 2
        if h + 1 < NHEADS:
            emit_loads_small(h + 1)
        for b in range(4 if STEP >= 1 else 0):
            P.op("dve", lambda e, b=b: e.memset(pbank[b][:], 0.0), writes=[PB[b]])
        tiles = []
        for k in range(8):
            for kc in range(2):
                tiles.append(("own", k, kc, k * 256, (k + 1) * 256))
        for j in range(int(os.environ.get('ATT_J', 63))):
            kmin = (j + 1) // 8
            r, i = block_loc(j)
            cp = 8 * i + r
            for kc in range(2):
                for b in range(kmin // 2, 4):
                    lo = max(kmin, 2 * b)
                    tiles.append(("past", cp, kc, lo * 256, (2 * b + 2) * 256, i))
        if STEP < 2:
            tiles = []
        LAG = int(os.environ.get('ATT_LAG', 2))
        pend = []

        last_of = {}
        for ti, tl in enumerate(tiles):
            if tl[0] == "past":
                last_of[tl[5]] = ti
        last_at = {ti: i for i, ti in last_of.items()}

        def emit_pv(vl, pt, ab, o0, n_, rds, ti):
            P.op("pe", lambda e: e.matmul(pbank[ab][0:65, o0:o0 + n_], lhsT=vl, rhs=PT[pt][:, 0:n_], start=False, stop=True, skip_group_check=True),
                 reads=rds + [BPT[pt]], writes=[PB[ab]])
            if ti in last_at and h + 1 < NHEADS:
                emit_loads_oct(h + 1, last_at[ti])

        for ti, tl in enumerate(tiles):
            sb = 4 + ntile % 3
            pt = ntile % 4
            ntile += 1
            if tl[0] == "own":
                _, k, kc, c0, c1 = tl
                n_ = c1 - c0
                P.op("pe", lambda e, k=k, kc=kc, c0=c0, c1=c1, sb=sb, n_=n_: e.matmul(pbank[sb][:, 0:n_], lhsT=KO[hi][:, k * 256 + kc * 128:k * 256 + (kc + 1) * 128], rhs=QA[hi][:, c0:c1], start=True, stop=False),
                     reads=[BKO[hi], BQA[hi]], writes=[PB[sb]], signal=False)
                P.op("pe", lambda e, kc=kc, sb=sb, n_=n_: e.matmul(pbank[sb][:, 0:n_], lhsT=identb, rhs=cb[:, kc, :], start=False, stop=True),
                     reads=[Bident, Bcb], writes=[PB[sb]])
                vl = VO[hi][:, k * 2 + kc, 0:65]
                rds = [BVO[hi]]
            else:
                _, cp, kc, c0, c1, i = tl
                n_ = c1 - c0
                P.op("pe", lambda e, cp=cp, kc=kc, c0=c0, c1=c1, sb=sb, n_=n_: e.matmul(pbank[sb][:, 0:n_], lhsT=KT[:, cp * 256 + kc * 128:cp * 256 + (kc + 1) * 128], rhs=QA[hi][:, c0:c1], start=True, stop=True),
                     reads=[BKT[i], Bsel, BQA[hi]], writes=[PB[sb]])
                vl = VA[:, cp * 2 + kc, 0:65]
                rds = [BVA[i], Bva1]
            if os.environ.get('EXP_SKIP', '0') == '0':
              P.op("act", lambda e, sb=sb, pt=pt, n_=n_: e.activation(out=PT[pt][:, 0:n_], in_=pbank[sb][:, 0:n_], func=getattr(AF, os.environ.get('EXP_FUNC', 'Exp')), scale=float(os.environ.get('EXP_SCALE', 1.0))), reads=[PB[sb]], writes=[BPT[pt]])
            ab = c0 // 512
            o0 = c0 % 512
            if STEP < 3:
                continue
            pend.append((vl, pt, ab, o0, n_, rds, ti))
            if len(pend) > LAG:
                emit_pv(*pend.pop(0))
        while pend:
            emit_pv(*pend.pop(0))
        for b in range(int(os.environ.get('ATT_NORM', 4))):
            P.op("dve", lambda e, b=b: e.reciprocal(out=rec[64:65, :], in_=pbank[b][64:65, :]), reads=[PB[b]], writes=[Brec])
            P.op("dve", lambda e, b=b: e.tensor_copy(out=tmpo[0:64, :], in_=pbank[b][0:64, :]), reads=[PB[b]], writes=[Btmpo])
            P.op("pe", lambda e: e.matmul(pbank[7][0:64, :], lhsT=onesf[64:65, :], rhs=rec[64:65, :], start=True, stop=True), reads=[Bonesf, Brec], writes=[PB[7]])
            P.op("dve", lambda e, b=b: e.tensor_tensor(out=atth[hi][0:64, b * 512:(b + 1) * 512], in0=tmpo[0:64, :], in1=pbank[7][0:64, :], op=ALU.mult), reads=[Btmpo, PB[7]], writes=[Batth[hi]])
        P.dma("sp", attT[hi * 64:(hi + 1) * 64, h // 2, :], atth[hi][0:64, :], reads=[Batth[hi]], writes=[BattT[h // 2]])
    P.barrier()

    if UPTO == "C":
        return
    A.off = reg_off
    g['warm_act'](AF.Sigmoid)
    wst = [A.alloc([8, 128], BF16) for _ in range(3)]
    wao = A.alloc([8, 1024], BF16); Bwao = Buf("wao", P.dsem())
    wo = A.alloc([8, 1024], BF16); Bwo = Buf("wo", P.dsem())
    P.dma("pool", wao, g["wao_d"].rearrange("(c p) f -> p c f", p=128), writes=[Bwao])
    P.dma("pool", wo, g["wo_d"].rearrange("(c p) f -> p c f", p=128), writes=[Bwo])
    sga = [A.alloc([512], F32) for _ in range(2)]; Bsga = P.bufs(2, "sga")
    tmpa = [A.alloc([512], F32) for _ in range(2)]; Btmpa = P.bufs(2, "tmpa")
    L = g["alloc_ln"]()
    g["load_ln_params"](L, 1)
    nsg = 0
    for G in range(4):
        for dc in range(8):
            for c in range(8):
                P.op("pe", lambda e, c=c, dc=dc: e.matmul(pbank[0][:], lhsT=wao[:, c, dc * 128:(dc + 1) * 128], rhs=attT[:, c, G * 512:(G + 1) * 512], start=(c == 0), stop=(c == 7)),
                     reads=[Bwao, BattT[c]], writes=[PB[0]], signal=(c == 7))
            w = load_w(6144 + dc * 128)
            fm_mm(w, 1, G)
            si = nsg % 2
            nsg += 1
            P.op("act", lambda e, si=si: e.activation(out=sga[si], in_=pbank[1][:], func=AF.Sigmoid), reads=[PB[1]], writes=[Bsga[si]])
            P.op("dve", lambda e, si=si: e.tensor_tensor(out=tmpa[si], in0=pbank[0][:], in1=sga[si], op=ALU.mult), reads=[PB[0], Bsga[si]], writes=[Btmpa[si]])
            P.op("dve", lambda e, si=si, dc=dc: e.tensor_tensor(out=AT[:, dc, G * 512:(G + 1) * 512], in0=AT[:, dc, G * 512:(G + 1) * 512], in1=tmpa[si], op=ALU.add), reads=[Btmpa[si], BAT[G]], writes=[BAT[G]])
    for t in range(NT):
        yb = [2 + 3 * (t % 2), 3 + 3 * (t % 2)]
        for h in range(2):
            for c in range(8):
                P.op("pe", lambda e, c=c, h=h, t=t, bank=yb[h]: e.matmul(pbank[bank][:], lhsT=AT[:, c, t * 128:(t + 1) * 128], rhs=wo[:, c, h * 512:(h + 1) * 512], start=(c == 0), stop=(c == 7)),
                     reads=[BAT[t // 4], Bwo], writes=[PB[yb[h]]], signal=(c == 7))
        g["ln_tile"](L, t, yb, g["x1_d"], g["x2_d"] if g["stage"] > 2 else g["out_d"], 4 + 3 * (t % 2), final=(g["stage"] == 2))
    P.barrier()


def host_consts(c):
    blks = blocks_of(c)
    pos = np.concatenate([np.arange(b * 256, (b + 1) * 256) for b in blks]).astype(np.float64)
    inv = 500000.0 ** (-np.arange(0, 16, 2, dtype=np.float32) / 16.0)
    ang = pos.astype(np.float32)[:, None] * inv[None, :].astype(np.float32)
    cos, sin = np.cos(ang).T, np.sin(ang).T
    cosT = np.ones((128, T), np.float32)
    sinT = np.zeros((128, T), np.float32)
    for hh in range(2):
        cosT[hh * 64:hh * 64 + 8] = cos
        cosT[hh * 64 + 8:hh * 64 + 16] = cos
        sinT[hh * 64:hh * 64 + 8] = -sin
        sinT[hh * 64 + 8:hh * 64 + 16] = sin
    biasA = np.zeros((8, 64), np.float32)
    biasB = np.zeros((8, 64), np.float32)
    for k in range(8):
        own = blks[k]
        for j in range(64):
            r, i = block_loc(j)
            gp = r * 8 + i
            biasA[k, gp] = 0.0 if j < own else NEG
            biasB[k, gp] = -BIG if j < own else -2 * BIG
    biasA = np.broadcast_to(biasA.reshape(1, 512), (128, 512)).copy()
    biasB = np.broadcast_to(biasB.reshape(1, 512), (128, 512)).copy()
    return dict(cosT=cosT, sinT=sinT, biasA=biasA, biasB=biasB)


def shared_consts():
    bf = ml_dtypes.bfloat16
    pmat = np.zeros((128, 128), np.float32)
    for hh in range(2):
        for m in range(16):
            partner = m + 8 if m < 8 else m - 8
            pmat[hh * 64 + partner, hh * 64 + m] = 1.0
    s_idx = np.arange(128)
    trimask = (s_idx[:, None] <= s_idx[None, :]).astype(np.float32)
    cb = np.zeros((128, 2, 256), np.float32)
    for kc in range(2):
        key = kc * 128 + np.arange(128)
        cb[:, kc, :] = np.where(key[:, None] <= np.arange(256)[None, :], 0.0, -BIG)
    sel = np.zeros((64, S), np.float32)
    for j in range(64):
        r, i = block_loc(j)
        cp = 8 * i + r
        sel[r * 8 + i, cp * 256:(cp + 1) * 256] = 1.0
    return dict(pmat=pmat, trimask=trimask, cb=np.ascontiguousarray(cb.reshape(128, 512)), sel=sel,
                identb=np.eye(128, dtype=np.float32), identf=np.eye(128, dtype=np.float32))


_NC_CACHE = {}


def kernel(x, ffn1_w_gate, ffn1_w_up, ffn1_w_down, ln1_g, ln1_b, w_in,
           gm_ln_g, gm_ln_b, gm_w_s, gm_b_s, w_gm_out, w_att_out, w_o,
           ln2_g, ln2_b, ffn2_w_gate, ffn2_w_up, ffn2_w_down, ln3_g, ln3_b, _stage=3):
    f = lambda a: np.ascontiguousarray(np.asarray(a, dtype=np.float32))
    x = f(x)[0]
    shared = dict(
        wg1=f(ffn1_w_gate)[0], wu1=f(ffn1_w_up)[0], wd1=f(ffn1_w_down)[0], ln1g=f(ln1_g)[0], ln1b=f(ln1_b)[0],
        win=f(w_in)[0], gmg=f(gm_ln_g)[0], gmb=f(gm_ln_b)[0], gmws=f(gm_w_s)[0], gmbs=f(gm_b_s)[0].reshape(1024),
        wgmo=f(w_gm_out)[0], wao=f(w_att_out)[0], wo=f(w_o)[0], ln2g=f(ln2_g)[0], ln2b=f(ln2_b)[0],
        wg2=f(ffn2_w_gate)[0], wu2=f(ffn2_w_up)[0], wd2=f(ffn2_w_down)[0], ln3g=f(ln3_g)[0], ln3b=f(ln3_b)[0])
    shared.update(shared_consts())
    in_maps = []
    for c in range(NCORE):
        blks = blocks_of(c)
        xc = np.concatenate([x[b * 256:(b + 1) * 256] for b in blks], axis=0)
        m = dict(shared)
        m["x"] = np.ascontiguousarray(xc)
        m.update(host_consts(c))
        in_maps.append(m)
    if _stage not in _NC_CACHE:
        _NC_CACHE[_stage] = build_program(_stage)
    nc = _NC_CACHE[_stage]
    res = run_bass_kernel_spmd(nc, in_maps, core_ids=list(range(NCORE)))
    out = np.empty((1, S, D), np.float32)
    for c in range(NCORE):
        oc = res.results[c]["out"]
        for k, b in enumerate(blocks_of(c)):
            out[0, b * 256:(b + 1) * 256] = oc[k * 256:(k + 1) * 256]
    return out
```

```python
import numpy as np
import ml_dtypes
from contextlib import ExitStack
import concourse.bass as bass
import concourse.mybir as mybir
from concourse.bass_utils import run_bass_kernel_spmd

F32 = mybir.dt.float32
BF16 = mybir.dt.bfloat16
AF = mybir.ActivationFunctionType
ALU = mybir.AluOpType
AX = mybir.AxisListType
ENG = ["pe", "act", "dve", "pool", "sp"]

NCORE = 8
S = 16384
T = 2048
NT = 16
D = 1024
DFF = 2816
NFC = 22
NH = 16
ALPHA = float(2.0 ** 0.25)
EPS = 1e-5
BIG = 30000.0
NEG = -1.0e30
ARENA = 207872


class Buf:
    __slots__ = ("w", "r", "name", "dsem", "excl")

    def __init__(self, name="", dsem=None, excl=False):
        self.w = None
        self.r = {}
        self.name = name
        self.dsem = dsem
        self.excl = excl


class DSem:
    __slots__ = ("sem", "count", "key")

    def __init__(self, sem, key):
        self.sem = sem
        self.count = 0
        self.key = key


class _Rec:
    def __init__(self):
        self.calls = []

    def __getattr__(self, name):
        def f(*a, **k):
            self.calls.append((name, a, k))
            return self
        return f


def _bind(fn):
    r = _Rec()
    fn(r)
    assert len(r.calls) == 1, r.calls
    name, a, k = r.calls[0]
    return lambda e: getattr(e, name)(*a, **k)


class Prog:
    def __init__(self, nc, stack):
        self.nc = nc
        self.stack = stack
        self.q = {e: [] for e in ENG}
        self.cnt = {e: 0 for e in ENG}
        self.pending = {e: False for e in ENG}
        self.known = {e: {} for e in ENG}
        self.sems = {e: stack.enter_context(nc.semaphore("s_" + e)) for e in ENG}
        self.semh = dict(self.sems)
        self.dsems = []
        self.trace = {e: [] for e in ENG}
        self.free = []
        self.inuse = []

    def dsem(self, persist=False):
        if not persist and self.free:
            d = self.free.pop()
            self.inuse.append(d)
            return d
        key = ("d", len(self.dsems))
        s = self.stack.enter_context(self.nc.semaphore("d%d" % len(self.dsems)))
        d = DSem(s, key)
        self.semh[key] = s
        self.dsems.append(d)
        if not persist:
            self.inuse.append(d)
        return d

    def bufs(self, n, name="", dma=False, excl=False, persist=False):
        return [Buf("%s%d" % (name, i), self.dsem(persist) if dma else None, excl) for i in range(n)]

    def _wait(self, eng, tok):
        if tok is None:
            return
        k, v = tok
        if self.known[eng].get(k, 0) >= v:
            return
        self.known[eng][k] = v
        self.trace[eng].append(('wait', k, v))
        sem = self.semh[k]
        self.q[eng].append(lambda e, sem=sem, v=v: e.wait_ge(sem, v))

    def _deps(self, eng, reads, writes):
        toks = []
        for b in reads:
            if b.w is not None:
                toks.append((b.w, "raw"))
        for b in writes:
            if b.w is not None:
                toks.append((b.w, "waw"))
            for k, v in b.r.items():
                toks.append(((k, v), "war"))
        for tok, kind in toks:
            k, v = tok
            if k == eng:
                if eng == "pe":
                    continue
                if kind == "war":
                    continue
                if v > self.cnt[eng]:
                    continue
            self._wait(eng, tok)

    def _commit(self, tok, reads, writes):
        for b in writes:
            b.w = tok
            b.r = {}
        for b in reads:
            if b in writes:
                continue
            k, v = tok
            if b.r.get(k, 0) < v:
                b.r[k] = v

    def op(self, eng, fn, reads=(), writes=(), signal=True):
        if any(b.excl for b in reads):
            writes = list(writes) + [b for b in reads if b.excl and b not in writes]
            reads = [b for b in reads if not b.excl]
        self._deps(eng, reads, writes)
        fn = _bind(fn)
        if signal:
            self.cnt[eng] += 1
            sem = self.sems[eng]
            self.q[eng].append(lambda e, fn=fn, sem=sem: fn(e).then_inc(sem, 1))
            self.pending[eng] = False
            tok = (eng, self.cnt[eng])
        else:
            self.q[eng].append(lambda e, fn=fn: fn(e))
            self.pending[eng] = True
            tok = (eng, self.cnt[eng] + 1)
        self.trace[eng].append(('op', tok, signal, [b.name for b in reads], [b.name for b in writes]))
        self._commit(tok, reads, writes)
        return tok

    def dma(self, eng, out_ap, in_ap, reads=(), writes=(), ds=None, inc=16, fn=None):
        if ds is None:
            for b in writes:
                if b.dsem is not None:
                    ds = b.dsem
                    break
        assert ds is not None
        self._deps(eng, reads, writes)
        ds.count += inc
        sem = ds.sem
        if fn is None:
            self.q[eng].append(
                lambda e, o=out_ap, i=in_ap, sem=sem, inc=inc: e.dma_start(out=o, in_=i).then_inc(sem, inc))
        else:
            fn = _bind(fn)
            self.q[eng].append(lambda e, fn=fn, sem=sem, inc=inc: fn(e).then_inc(sem, inc))
        tok = (ds.key, ds.count)
        self.trace[eng].append(('dma', tok, [b.name for b in reads], [b.name for b in writes]))
        self._commit(tok, reads, writes)
        return tok

    def barrier(self):
        for e in ENG:
            assert not self.pending[e], e
        for d in self.dsems:
            if d.count:
                self._wait("sp", (d.key, d.count))
        for o in ENG:
            if o != "sp" and self.cnt[o]:
                self._wait("sp", (o, self.cnt[o]))
        self.cnt["sp"] += 1
        sem = self.sems["sp"]
        self.q["sp"].append(lambda e, sem=sem: e.nop().then_inc(sem, 1))
        tok = ("sp", self.cnt["sp"])
        for e in ENG:
            if e != "sp":
                self._wait(e, tok)
            for o in ENG:
                self.known[e][o] = max(self.known[e].get(o, 0), self.cnt[o])
            for d in self.dsems:
                self.known[e][d.key] = max(self.known[e].get(d.key, 0), d.count)
        self.free.extend(self.inuse)
        self.inuse = []

    def final_wait(self, toks):
        for t in toks:
            self._wait("sp", t)

    def emit(self):
        nc = self.nc
        q = self.q
        with nc.Block() as block:
            @block.tensor
            def _(e):
                for f in q["pe"]:
                    f(e)

            @block.scalar
            def _(e):
                for f in q["act"]:
                    f(e)

            @block.vector
            def _(e):
                for f in q["dve"]:
                    f(e)

            @block.gpsimd
            def _(e):
                for f in q["pool"]:
                    f(e)

            @block.sync
            def _(e):
                for f in q["sp"]:
                    f(e)


class Arena:
    def __init__(self, t):
        self.t = t
        self.off = 0

    def alloc(self, shape, dt):
        n = int(np.prod(shape))
        nb = n * (4 if dt == F32 else 2)
        nb = (nb + 63) // 64 * 64
        assert self.off + nb <= ARENA, (self.off, nb)
        ap = self.t[:, self.off // 2:(self.off + nb) // 2]
        self.off += nb
        if dt == F32:
            ap = ap.bitcast(F32)
        ap = ap[:, 0:n]
        if len(shape) == 2:
            ap = ap.rearrange("p (a b) -> p a b", a=shape[0])
        elif len(shape) == 3:
            ap = ap.rearrange("p (a b c) -> p a b c", a=shape[0], b=shape[1])
        return ap


def blocks_of(c):
    out = []
    for m in range(4):
        out.append(16 * m + c)
        out.append(16 * m + 15 - c)
    return out


def block_loc(j):
    m, t = divmod(j, 16)
    if t < 8:
        return t, 2 * m
    return 15 - t, 2 * m + 1


def build_program(stage=3):
    nc = bass.Bass("TRN2", target_bir_lowering=False)

    def din(name, shape, dt=F32):
        return nc.dram_tensor(name, shape, dt, kind="ExternalInput").ap()

    def dint(name, shape, dt):
        return nc.dram_tensor(name, shape, dt).ap()

    x_d = din("x", [T, D])
    wg_d = [din("wg1", [D, DFF]), din("wg2", [D, DFF])]
    wu_d = [din("wu1", [D, DFF]), din("wu2", [D, DFF])]
    wd_d = [din("wd1", [DFF, D]), din("wd2", [DFF, D])]
    lng_d = [din("ln1g", [D]), din("ln2g", [D]), din("ln3g", [D])]
    lnb_d = [din("ln1b", [D]), din("ln2b", [D]), din("ln3b", [D])]
    win_d = din("win", [D, 7168])
    gmg_d, gmb_d = din("gmg", [D]), din("gmb", [D])
    gmws_d = din("gmws", [8, 128, 128])
    gmbs_d = din("gmbs", [1024])
    wgmo_d, wao_d, wo_d = din("wgmo", [D, D]), din("wao", [D, D]), din("wo", [D, D])
    cos_d, sin_d = din("cosT", [128, T]), din("sinT", [128, T])
    pmat_d = din("pmat", [128, 128])
    tri_d = din("trimask", [128, 128])
    cb_d = din("cb", [128, 512])
    sel_d = din("sel", [64, S])
    biasA_d, biasB_d = din("biasA", [128, 512]), din("biasB", [128, 512])
    identb_d = din("identb", [128, 128])
    identf_d = din("identf", [128, 128])
    out_d = nc.dram_tensor("out", [T, D], F32, kind="ExternalOutput").ap()

    x1_d = dint("x1_d", [T, D], F32)
    x2_d = dint("x2_d", [T, D], F32)
    q_d = dint("q_d", [1024, T], BF16)
    mb_d = dint("mb_d", [1024, T], BF16)
    agk_in_f = dint("agk_in", [1024, T // 2], F32)
    agv_in_f = dint("agv_in", [T, 512], F32)
    agk_out_f = dint("agk_out", [8 * 1024, T // 2], F32)
    agv_out_f = dint("agv_out", [8 * T, 512], F32)
    agk_in, agv_in, agk_out, agv_out = (a.bitcast(BF16) for a in (agk_in_f, agv_in_f, agk_out_f, agv_out_f))
    print("bitcast shapes", agk_in.shape, agv_out.shape)
    km_in = dint("km_in", [128, 64], F32)
    km_out = dint("km_out", [8 * 128, 64], F32)

    with ExitStack() as st:
        P = Prog(nc, st)
        arena_t = st.enter_context(nc.sbuf_tensor("arena", [128, ARENA // 2], BF16))
        pbank = [st.enter_context(nc.psum_tensor("pb%d" % i, [128, 512], F32)) for i in range(8)]
        PB = P.bufs(8, "pb", excl=True)
        A = Arena(arena_t)

        def pbf(i):
            return pbank[i][:].bitcast(BF16)

        xT = A.alloc([8, T], BF16)
        XT = P.bufs(2, "xT")
        identb = A.alloc([128], BF16)
        Bident = Buf("ident", P.dsem(True))
        P.dma("pool", identb, identb_d, writes=[Bident])
        base_off = A.off

        out_toks = []

        def transpose_into_xT(xb_ap, Bxb, t, bank):
            pv = pbf(bank)
            for c in range(8):
                P.op("pe", lambda e, c=c: e.transpose(out=pv[:, c * 128:(c + 1) * 128], in_=xb_ap[:, c * 128:(c + 1) * 128], identity=identb),
                     reads=[Bxb, Bident], writes=[PB[bank]], signal=(c == 7))
            P.op("dve", lambda e: e.tensor_copy(out=xT[:, :, t * 128:(t + 1) * 128], in_=pv.rearrange("p (a b) -> p a b", a=8)),
                 reads=[PB[bank]], writes=[XT[t // 8]])

        class LNBufs:
            pass

        def alloc_ln():
            L = LNBufs()
            L.xtm = [A.alloc([D], F32) for _ in range(2)]
            L.Bxtm = P.bufs(2, "xtm", dma=True)
            L.z = A.alloc([D], F32)
            L.Bz = Buf("z")
            L.xo = [A.alloc([D], F32) for _ in range(2)]
            L.Bxo = P.bufs(2, "xo")
            L.xb = [A.alloc([D], BF16) for _ in range(2)]
            L.Bxb = P.bufs(2, "xb")
            L.gbc = A.alloc([D], F32)
            L.bbc = A.alloc([D], F32)
            L.Bgb = Buf("gb", P.dsem())
            L.st = A.alloc([16], F32)
            L.Bst = Buf("st")
            L.dst = [P.dsem(True) for _ in range(2)]
            L.n = 0
            return L

        def load_ln_params(L, li):
            P.dma("sp", L.gbc, lng_d[li].partition_broadcast(128), writes=[L.Bgb])
            P.dma("sp", L.bbc, lnb_d[li].partition_broadcast(128), writes=[L.Bgb])

        def ln_tile(L, t, ybanks, res_d, dst_d, tr_bank, final=False):
            i = L.n % 2
            L.n += 1
            P.dma("sp", L.xtm[i], res_d[t * 128:(t + 1) * 128, :], writes=[L.Bxtm[i]])
            for h in range(2):
                P.op("dve", lambda e, h=h: e.scalar_tensor_tensor(out=L.z[:, h * 512:(h + 1) * 512], in0=L.xtm[i][:, h * 512:(h + 1) * 512], scalar=ALPHA,
                                                                in1=pbank[ybanks[h]][:], op0=ALU.mult, op1=ALU.add),
                     reads=[L.Bxtm[i], PB[ybanks[h]]], writes=[L.Bz])
            for h in range(2):
                P.op("dve", lambda e, h=h: e.bn_stats(out=L.st[:, h * 6:(h + 1) * 6], in_=L.z[:, h * 512:(h + 1) * 512]), reads=[L.Bz], writes=[L.Bst])
            P.op("dve", lambda e: e.bn_aggr(out=L.st[:, 12:14], in_=L.st[:, 0:12]), reads=[L.Bst], writes=[L.Bst])
            P.op("act", lambda e: e.activation(out=L.st[:, 14:15], in_=L.st[:, 13:14], func=AF.Sqrt, bias=EPS), reads=[L.Bst], writes=[L.Bst])
            P.op("dve", lambda e: e.reciprocal(out=L.st[:, 15:16], in_=L.st[:, 14:15]), reads=[L.Bst], writes=[L.Bst])
            P.op("dve", lambda e: e.tensor_scalar(out=L.z[:], in0=L.z[:], scalar1=L.st[:, 12:13], scalar2=L.st[:, 15:16], op0=ALU.subtract, op1=ALU.mult),
                 reads=[L.Bz, L.Bst], writes=[L.Bz])
            P.op("dve", lambda e: e.tensor_tensor(out=L.z[:], in0=L.z[:], in1=L.gbc, op=ALU.mult), reads=[L.Bz, L.Bgb], writes=[L.Bz])
            P.op("dve", lambda e: e.tensor_tensor(out=L.xo[i], in0=L.z[:], in1=L.bbc, op=ALU.add), reads=[L.Bz, L.Bgb], writes=[L.Bxo[i]])
            tok = P.dma("sp", dst_d[t * 128:(t + 1) * 128, :], L.xo[i], reads=[L.Bxo[i]], ds=L.dst[i])
            if final:
                out_toks.append(tok)
            else:
                P.op("act", lambda e: e.copy(out=L.xb[i], in_=L.xo[i]), reads=[L.Bxo[i]], writes=[L.Bxb[i]])
                transpose_into_xT(L.xb[i], L.Bxb[i], t, tr_bank)

        def ffn_phase(fi, li, res_d, dst_d, first, final):
            A.off = base_off
            hT = A.alloc([NFC, 1024], BF16)
            HT = P.bufs(NFC, "hT")
            wd = A.alloc([NFC, D], BF16)
            WD = P.bufs(NFC, "wd")
            dwd = P.dsem()
            for b_ in WD:
                b_.dsem = dwd
            wgu = [[A.alloc([8, 128], BF16) for _ in range(2)] for _ in range(3)]
            WGU = [P.bufs(2, "wgu", dma=True) for _ in range(3)]
            sg = [A.alloc([512], F32) for _ in range(4)]
            SG = P.bufs(4, "sg")
            L = alloc_ln()
            load_ln_params(L, li)
            if first:
                for t in range(NT):
                    i = t % 2
                    P.dma("sp", L.xtm[i], x_d[t * 128:(t + 1) * 128, :], writes=[L.Bxtm[i]])
                    P.op("act", lambda e, i=i: e.copy(out=L.xb[i], in_=L.xtm[i]), reads=[L.Bxtm[i]], writes=[L.Bxb[i]])
                    transpose_into_xT(L.xb[i], L.Bxb[i], t, t % 4)
            wgr = wg_d[fi].rearrange("(c p) f -> p c f", p=128)
            wur = wu_d[fi].rearrange("(c p) f -> p c f", p=128)
            for fc in range(NFC):
                P.dma("pool", wd[:, fc, :], wd_d[fi][fc * 128:(fc + 1) * 128, :], writes=[WD[fc]])
            nw = 0
            nsg = 0
            for stt in range(2):
                for fc in range(NFC):
                    w = nw % 3
                    nw += 1
                    P.dma("pool", wgu[w][0], wgr[:, :, fc * 128:(fc + 1) * 128], writes=[WGU[w][0]])
                    P.dma("pool", wgu[w][1], wur[:, :, fc * 128:(fc + 1) * 128], writes=[WGU[w][1]])
                    bset = 4 * (fc % 2)
                    for h in range(2):
                        cols = slice(stt * 1024 + h * 512, stt * 1024 + (h + 1) * 512)
                        for gu in range(2):
                            bank = bset + 2 * gu + h
                            for c in range(8):
                                P.op("pe", lambda e, bank=bank, c=c, w=w, gu=gu, cols=cols: e.matmul(pbank[bank][:], lhsT=wgu[w][gu][:, c, :], rhs=xT[:, c, cols], start=(c == 0), stop=(c == 7)),
                                     reads=[WGU[w][gu], XT[stt]], writes=[PB[bank]], signal=(c == 7))
                        si = nsg % 4
                        nsg += 1
                        P.op("act", lambda e, si=si, bank=bset + h: e.activation(out=sg[si], in_=pbank[bank][:], func=AF.Silu), reads=[PB[bset + h]], writes=[SG[si]])
                        P.op("dve", lambda e, si=si, bank=bset + 2 + h, fc=fc, h=h: e.scalar_tensor_tensor(out=hT[:, fc, h * 512:(h + 1) * 512], in0=pbank[bank][:], scalar=0.5, in1=sg[si],
                                                                                                   op0=ALU.mult, op1=ALU.mult),
                             reads=[PB[bset + 2 + h], SG[si]], writes=[HT[fc]])
                for tt in range(8):
                    t = stt * 8 + tt
                    yb = [4 * (tt % 2), 4 * (tt % 2) + 1]
                    for h in range(2):
                        for fc in range(NFC):
                            P.op("pe", lambda e, h=h, fc=fc, tt=tt, bank=yb[h]: e.matmul(pbank[bank][:], lhsT=hT[:, fc, tt * 128:(tt + 1) * 128], rhs=wd[:, fc, h * 512:(h + 1) * 512],
                                                                                     start=(fc == 0), stop=(fc == NFC - 1)),
                                 reads=[HT[fc], WD[fc]], writes=[PB[yb[h]]], signal=(fc == NFC - 1))
                    ln_tile(L, t, yb, res_d, dst_d, 4 * (tt % 2) + 2, final=final)
            P.barrier()

        ffn_phase(0, 0, x_d, x1_d if stage > 1 else out_d, True, stage == 1)

        if stage >= 2:
            mixer_phase(nc, P, A, base_off, locals())
        if stage >= 3:
            ffn_phase(1, 2, x2_d, out_d, False, True)

        P.final_wait(out_toks)
        P.emit()
    return nc


def mixer_phase(nc, P, A, base_off, env):
    import os
    g = env
    xT, XT, pbank, PB, pbf, identb, Bident = g["xT"], g["XT"], g["pbank"], g["PB"], g["pbf"], g["identb"], g["Bident"]
    win_d = g["win_d"]
    winr = win_d.rearrange("(c p) f -> p c f", p=128)
    A.off = base_off
    AT = A.alloc([8, T], BF16)
    BAT = P.bufs(4, "AT")
    attT = A.alloc([8, T], BF16)
    BattT = P.bufs(8, "attT", dma=True, persist=True)
    reg_off = A.off
    cosT = A.alloc([T], F32); sinT = A.alloc([T], F32)
    Bcs = Buf("cs", P.dsem())
    P.dma("sp", cosT, g["cos_d"], writes=[Bcs]); P.dma("sp", sinT, g["sin_d"], writes=[Bcs])
    pmat = A.alloc([128], BF16); Bpm = Buf("pm", P.dsem())
    P.dma("pool", pmat, g["pmat_d"], writes=[Bpm])
    wst = [A.alloc([8, 128], BF16) for _ in range(3)]; WST = P.bufs(3, "wst", dma=True)
    wtm = A.alloc([8, 1024], BF16); Bwtm = Buf("wtm", P.dsem())
    kraw = [A.alloc([512], BF16) for _ in range(2)]; Bkraw = P.bufs(2, "kraw")
    t1 = [A.alloc([512], F32) for _ in range(2)]; Bt1 = P.bufs(2, "t1")
    t2 = [A.alloc([512], F32) for _ in range(2)]; Bt2 = P.bufs(2, "t2")
    krot = [A.alloc([512], F32) for _ in range(2)]; Bkrot = P.bufs(2, "krot")
    kbf = [A.alloc([512], BF16) for _ in range(2)]; Bkbf = P.bufs(2, "kbf"); dkbf = [P.dsem() for _ in range(2)]
    vsb = [A.alloc([1024], BF16) for _ in range(2)]; Bvsb = P.bufs(2, "vsb"); dvsb = [P.dsem() for _ in range(2)]
    kst = A.alloc([16], F32); Bkst = Buf("kst")
    kmloc = A.alloc([8, 8], F32); Bkmloc = Buf("kmloc")
    kmall2 = A.alloc([8, 64], F32); Bkm2 = Buf("km2", P.dsem())
    kmall = A.alloc([8, 64], F32); Bkmall = Buf("kmall")
    bA = A.alloc([512], F32); bB = A.alloc([512], F32); BbAB = Buf("bAB", P.dsem())
    P.dma("sp", bA, g["biasA_d"], writes=[BbAB]); P.dma("sp", bB, g["biasB_d"], writes=[BbAB])
    Gm = A.alloc([64], F32); BGm = Buf("Gm")
    mx8 = A.alloc([8], F32); Bmx = Buf("mx")
    MBq = [A.alloc([128], BF16) for _ in range(2)]; BMBq = P.bufs(2, "MBq")
    mbT = [A.alloc([512], BF16) for _ in range(2)]; BmbT = P.bufs(2, "mbT"); dmbT = [P.dsem() for _ in range(2)]
    for i in range(2):
        P.op("dve", lambda e, i=i: e.memset(MBq[i], 0.0), writes=[BMBq[i]])
    nws = [0]
    Bagk = Buf("agk", P.dsem()); Bagv = Buf("agv", P.dsem()); Bkmin = Buf("kmin", P.dsem())
    Bagko = Buf("agko", P.dsem()); Bagvo = Buf("agvo", P.dsem()); Bkmo = Buf("kmo", P.dsem())
    Bqd = Buf("qd"); Bmbd = Buf("mbd")
    grp = [list(range(NCORE))]

    def proj_fm(col0, bank, tt):
        w = nws[0] % 3
        return w

    def load_w(col0):
        w = nws[0] % 3
        nws[0] += 1
        P.dma("pool", wst[w], winr[:, :, col0:col0 + 128], writes=[WST[w]])
        return w

    def fm_mm(w, bank, tt):
        for c in range(8):
            P.op("pe", lambda e, c=c: e.matmul(pbank[bank][:], lhsT=wst[w][:, c, :], rhs=xT[:, c, tt * 512:(tt + 1) * 512], start=(c == 0), stop=(c == 7)),
                 reads=[WST[w], XT[tt // 2]], writes=[PB[bank]], signal=(c == 7))

    def rope(bank, bank2, tt, n, scale):
        i = n % 2
        P.op("act", lambda e: e.copy(out=kraw[i], in_=pbank[bank][:]), reads=[PB[bank]], writes=[Bkraw[i]])
        P.op("pe", lambda e: e.matmul(pbank[bank2][:], lhsT=pmat, rhs=kraw[i], start=True, stop=True), reads=[Bpm, Bkraw[i]], writes=[PB[bank2]])
        cs = slice(tt * 512, (tt + 1) * 512)
        P.op("dve", lambda e: e.tensor_tensor(out=t1[i], in0=pbank[bank][:], in1=cosT[:, cs], op=ALU.mult), reads=[PB[bank], Bcs], writes=[Bt1[i]])
        P.op("dve", lambda e: e.tensor_tensor(out=t2[i], in0=pbank[bank2][:], in1=sinT[:, cs], op=ALU.mult), reads=[PB[bank2], Bcs], writes=[Bt2[i]])
        P.op("dve", lambda e: e.tensor_tensor(out=krot[i], in0=t1[i], in1=t2[i], op=ALU.add), reads=[Bt1[i], Bt2[i]], writes=[Bkrot[i]])
        P.op("act", lambda e: e.activation(out=kbf[i], in_=krot[i], func=AF.Copy, scale=scale), reads=[Bkrot[i]], writes=[Bkbf[i]])
        return i

    n = 0
    for pr in range(8):
        w = load_w(3072 + pr * 128)
        for tt in range(4):
            bank = (n % 2) * 2
            fm_mm(w, bank, tt)
            i = rope(bank, bank + 1, tt, n, 1.0)
            n += 1
            P.dma("sp", g["agk_in"][pr * 128:(pr + 1) * 128, tt * 512:(tt + 1) * 512], kbf[i], reads=[Bkbf[i]], writes=[Bagk], ds=dkbf[i])
            for hb in range(2):
                P.op("dve", lambda e, hb=hb, i=i: e.bn_stats(out=kst[:, 0:6], in_=krot[i][:, hb * 256:(hb + 1) * 256]), reads=[Bkrot[i]], writes=[Bkst])
                P.op("dve", lambda e, hb=hb, pr=pr, tt=tt: e.bn_aggr(out=kst[:, 8:10], in_=kst[:, 0:6]), reads=[Bkst], writes=[Bkst])
                P.op("dve", lambda e, hb=hb, pr=pr, tt=tt: e.tensor_copy(out=kmloc[:, pr, tt * 2 + hb:tt * 2 + hb + 1], in_=kst[:, 8:9]), reads=[Bkst], writes=[Bkmloc])
    P.dma("sp", g["km_in"], kmloc.rearrange("p a b -> p (a b)"), reads=[Bkmloc], writes=[Bkmin])
    NOCC = os.environ.get("NOCC", "0") == "1"
    if not NOCC:
      P.dma("pool", None, None, reads=[Bkmin], writes=[Bkmo], inc=1,
          fn=lambda e: e.collective_compute("AllGather", ALU.bypass, replica_groups=grp, ins=[g["km_in"]], outs=[g["km_out"]]))
    for d_ in dkbf:
        P._wait("pool", (d_.key, d_.count))
    if not NOCC:
      P.dma("pool", None, None, reads=[Bagk], writes=[Bagko], inc=1,
          fn=lambda e: e.collective_compute("AllGather", ALU.bypass, replica_groups=grp, ins=[g["agk_in_f"]], outs=[g["agk_out_f"]]))
    P.dma("pool", wtm, winr[:, :, 4096:5120], writes=[Bwtm])
    for t in range(NT):
        i = t % 2
        for h in range(2):
            bank = 4 + h
            for c in range(8):
                P.op("pe", lambda e, c=c, h=h, t=t, bank=bank: e.matmul(pbank[bank][:], lhsT=xT[:, c, t * 128:(t + 1) * 128], rhs=wtm[:, c, h * 512:(h + 1) * 512], start=(c == 0), stop=(c == 7)),
                     reads=[Bwtm, XT[t // 8]], writes=[PB[bank]], signal=(c == 7))
            P.op("act", lambda e, h=h, i=i, bank=bank: e.copy(out=vsb[i][:, h * 512:(h + 1) * 512], in_=pbank[bank][:]), reads=[PB[bank]], writes=[Bvsb[i]])
        P.dma("sp", g["agv_in"][t * 128:(t + 1) * 128, :], vsb[i], reads=[Bvsb[i]], writes=[Bagv], ds=dvsb[i])
    for d_ in dvsb:
        P._wait("pool", (d_.key, d_.count))
    if not NOCC:
      P.dma("pool", None, None, reads=[Bagv], writes=[Bagvo], inc=1,
          fn=lambda e: e.collective_compute("AllGather", ALU.bypass, replica_groups=grp, ins=[g["agv_in_f"]], outs=[g["agv_out_f"]]))
    P.dma("sp", kmall2.rearrange("p a b -> p (a b)").rearrange("p (r c) -> p r c", r=8), g["km_out"].rearrange("(r p) c -> p r c", p=128), reads=[Bkmo], writes=[Bkm2])
    for pr in range(8):
        P.op("dve", lambda e, pr=pr: e.tensor_copy(out=kmall[:, pr, :].rearrange("p (r i) -> p r i", r=8), in_=kmall2[:, :, pr * 8:(pr + 1) * 8]), reads=[Bkm2], writes=[Bkmall])
    for pr in range(8):
        w = load_w(2048 + pr * 128)
        for tt in range(4):
            bank = (n % 2) * 2
            fm_mm(w, bank, tt)
            i = rope(bank, bank + 1, tt, n, 0.125)
            n += 1
            P.dma("sp", g["q_d"][pr * 128:(pr + 1) * 128, tt * 512:(tt + 1) * 512], kbf[i], reads=[Bkbf[i]], writes=[Bqd], ds=dkbf[i])
            for par in range(2):
                hb = par * 64
                mi = (n * 2 + par) % 2
                for qt in range(4):
                    slot = (tt * 4 + qt) // 2
                    P.op("pe", lambda e, hb=hb, qt=qt, i=i, pr=pr: e.matmul(pbank[6][:, 0:64], lhsT=krot[i][hb:hb + 64, qt * 128:(qt + 1) * 128], rhs=kmall[hb:hb + 64, pr, :], start=True, stop=True),
                         reads=[Bkrot[i], Bkmall], writes=[PB[6]])
                    P.op("dve", lambda e, slot=slot: e.tensor_tensor(out=Gm, in0=pbank[6][:, 0:64], in1=bA[:, slot * 64:(slot + 1) * 64], op=ALU.add), reads=[PB[6], BbAB], writes=[BGm])
                    P.op("dve", lambda e: e.max(out=mx8, in_=Gm), reads=[BGm], writes=[Bmx])
                    P.op("dve", lambda e: e.tensor_scalar(out=Gm, in0=Gm, scalar1=mx8[:, 2:3], scalar2=BIG, op0=ALU.is_ge, op1=ALU.mult), reads=[BGm, Bmx], writes=[BGm])
                    mq = qt % 2
                    P.op("dve", lambda e, slot=slot, mq=mq: e.tensor_tensor(out=MBq[mq][:, 64:128], in0=Gm, in1=bB[:, slot * 64:(slot + 1) * 64], op=ALU.add), reads=[BGm, BbAB], writes=[BMBq[mq]])
                    pv = pbf(7)
                    P.op("pe", lambda e, mq=mq, pv=pv: e.transpose(out=pv[:, 0:128], in_=MBq[mq], identity=identb), reads=[BMBq[mq], Bident], writes=[PB[7]])
                    P.op("act", lambda e, mi=mi, qt=qt, pv=pv: e.copy(out=mbT[mi][64:128, qt * 128:(qt + 1) * 128], in_=pv[64:128, 0:128]), reads=[PB[7]], writes=[BmbT[mi]])
                h = pr * 2 + par
                P.dma("sp", g["mb_d"][h * 64:(h + 1) * 64, tt * 512:(tt + 1) * 512], mbT[mi][64:128, :], reads=[BmbT[mi]], writes=[Bmbd], ds=dmbT[mi])
    P.barrier()

    import os
    UPTO = os.environ.get("MIX_UPTO", "Z")
    if UPTO == "A":
        return
    A.off = reg_off
    wst = [A.alloc([8, 128], BF16) for _ in range(3)]
    wtm = A.alloc([8, 1024], BF16)
    wgmo = A.alloc([8, 1024], BF16); Bwgmo = Buf("wgmo", P.dsem())
    P.dma("pool", wtm, winr[:, :, 1024:2048], writes=[Bwtm])
    P.dma("pool", wgmo, g["wgmo_d"].rearrange("(c p) f -> p c f", p=128), writes=[Bwgmo])
    wsraw = A.alloc([8, 128], BF16); Bwsraw = Buf("wsraw", P.dsem())
    P.dma("pool", wsraw, g["gmws_d"].rearrange("g t s -> t g s"), writes=[Bwsraw])
    tri = A.alloc([128], F32); Btri = Buf("tri", P.dsem())
    P.dma("sp", tri, g["tri_d"], writes=[Btri])
    wsT = A.alloc([8, 128], BF16); BwsT = Buf("wsT")
    for gg in range(8):
        pv = pbf(7)
        P.op("pe", lambda e, gg=gg, pv=pv: e.transpose(out=pv[:, 0:128], in_=wsraw[:, gg, :], identity=identb), reads=[Bwsraw, Bident], writes=[PB[7]])
        P.op("dve", lambda e, gg=gg, pv=pv: e.tensor_tensor(out=wsT[:, gg, :], in0=pv[:, 0:128], in1=tri, op=ALU.mult), reads=[PB[7], Btri], writes=[BwsT])
    bsbc = A.alloc([8, 128], F32); gmgb = A.alloc([D], F32); gmbb = A.alloc([D], F32); Bgmp = Buf("gmp", P.dsem())
    P.dma("sp", bsbc.rearrange("p a b -> p (a b)"), g["gmbs_d"].partition_broadcast(128), writes=[Bgmp])
    P.dma("sp", gmgb, g["gmg_d"].partition_broadcast(128), writes=[Bgmp])
    P.dma("sp", gmbb, g["gmb_d"].partition_broadcast(128), writes=[Bgmp])
    ug = [A.alloc([8, 512], BF16) for _ in range(2)]; Bug = P.bufs(2, "ug")
    vg = A.alloc([D], F32); Bvg = Buf("vg")
    vn = [A.alloc([D], BF16) for _ in range(2)]; Bvn = P.bufs(2, "vn")
    gst = A.alloc([16], F32); Bgst = Buf("gst")
    svs = A.alloc([8, 128], F32); Bsvs = Buf("svs")
    mT = [A.alloc([8, 512], BF16) for _ in range(2)]; BmT = P.bufs(2, "mT")
    sgm = [A.alloc([512], F32) for _ in range(2)]; Bsgm = P.bufs(2, "sgm")
    nsg = 0
    for G in range(4):
        gi = G % 2
        for c8 in range(8):
            w = load_w(c8 * 128)
            bank = c8 % 2
            fm_mm(w, bank, G)
            P.op("act", lambda e, c8=c8, bank=bank: e.activation(out=ug[gi][:, c8, :], in_=pbank[bank][:], func=AF.Gelu_apprx_tanh), reads=[PB[bank]], writes=[Bug[gi]])
        for q4 in range(4):
            t = G * 4 + q4
            vi = t % 2
            for h in range(2):
                bank = 2 + h
                for c in range(8):
                    P.op("pe", lambda e, c=c, h=h, t=t, bank=bank: e.matmul(pbank[bank][:], lhsT=xT[:, c, t * 128:(t + 1) * 128], rhs=wtm[:, c, h * 512:(h + 1) * 512], start=(c == 0), stop=(c == 7)),
                         reads=[Bwtm, XT[t // 8]], writes=[PB[bank]], signal=(c == 7))
                P.op("act", lambda e, h=h, bank=bank: e.activation(out=vg[:, h * 512:(h + 1) * 512], in_=pbank[bank][:], func=AF.Gelu_apprx_tanh), reads=[PB[bank]], writes=[Bvg])
            for h in range(2):
                P.op("dve", lambda e, h=h: e.bn_stats(out=gst[:, h * 6:(h + 1) * 6], in_=vg[:, h * 512:(h + 1) * 512]), reads=[Bvg], writes=[Bgst])
            P.op("dve", lambda e: e.bn_aggr(out=gst[:, 12:14], in_=gst[:, 0:12]), reads=[Bgst], writes=[Bgst])
            P.op("act", lambda e: e.activation(out=gst[:, 14:15], in_=gst[:, 13:14], func=AF.Sqrt, bias=EPS), reads=[Bgst], writes=[Bgst])
            P.op("dve", lambda e: e.reciprocal(out=gst[:, 15:16], in_=gst[:, 14:15]), reads=[Bgst], writes=[Bgst])
            P.op("dve", lambda e: e.tensor_scalar(out=vg, in0=vg, scalar1=gst[:, 12:13], scalar2=gst[:, 15:16], op0=ALU.subtract, op1=ALU.mult), reads=[Bvg, Bgst], writes=[Bvg])
            P.op("dve", lambda e: e.tensor_tensor(out=vg, in0=vg, in1=gmgb, op=ALU.mult), reads=[Bvg, Bgmp], writes=[Bvg])
            P.op("dve", lambda e, vi=vi: e.tensor_tensor(out=vn[vi], in0=vg, in1=gmbb, op=ALU.add), reads=[Bvg, Bgmp], writes=[Bvn[vi]])
            for hb in range(2):
                bank = 4 + hb
                for g4 in range(4):
                    gg = hb * 4 + g4
                    P.op("pe", lambda e, gg=gg, g4=g4, vi=vi, bank=bank: e.matmul(pbank[bank][:, g4 * 128:(g4 + 1) * 128], lhsT=vn[vi][:, gg * 128:(gg + 1) * 128], rhs=wsT[:, gg, :], start=True, stop=True),
                         reads=[Bvn[vi], BwsT], writes=[PB[bank]], signal=(g4 == 3))
                P.op("dve", lambda e, hb=hb, bank=bank: e.tensor_tensor(out=svs[:, hb * 4:(hb + 1) * 4, :], in0=pbank[bank][:].rearrange("p (a b) -> p a b", a=4), in1=bsbc[:, hb * 4:(hb + 1) * 4, :], op=ALU.add),
                     reads=[PB[bank], Bgmp], writes=[Bsvs])
            P.op("dve", lambda e, q4=q4: e.tensor_tensor(out=mT[gi][:, :, q4 * 128:(q4 + 1) * 128], in0=svs, in1=ug[gi][:, :, q4 * 128:(q4 + 1) * 128], op=ALU.mult),
                 reads=[Bsvs, Bug[gi]], writes=[BmT[gi]])
        for dc in range(8):
            bank = 6
            for c in range(8):
                P.op("pe", lambda e, c=c, dc=dc: e.matmul(pbank[6][:], lhsT=wgmo[:, c, dc * 128:(dc + 1) * 128], rhs=mT[gi][:, c, :], start=(c == 0), stop=(c == 7)),
                     reads=[Bwgmo, BmT[gi]], writes=[PB[6]], signal=(c == 7))
            w = load_w(5120 + dc * 128)
            fm_mm(w, 7, G)
            si = nsg % 2
            nsg += 1
            P.op("act", lambda e, si=si: e.activation(out=sgm[si], in_=pbank[7][:], func=AF.Sigmoid), reads=[PB[7]], writes=[Bsgm[si]])
            P.op("dve", lambda e, si=si, dc=dc: e.tensor_tensor(out=AT[:, dc, G * 512:(G + 1) * 512], in0=pbank[6][:], in1=sgm[si], op=ALU.mult), reads=[PB[6], Bsgm[si]], writes=[BAT[G]])
    P.barrier()

    if UPTO == "B":
        return
    print('CNT before attention', P.cnt)
    P.mark = {e: len(P.trace[e]) for e in ENG}
    A.off = reg_off
    if os.environ.get('EXP_WARM', '1') == '1':
        dmy = A.alloc([64], F32); Bdmy = Buf('dmy')
        P.op('dve', lambda e: e.memset(dmy, 0.0), writes=[Bdmy])
        P.op('act', lambda e: e.activation(out=dmy, in_=dmy, func=AF.Exp), reads=[Bdmy], writes=[Bdmy])
        P.barrier()
    KT = A.alloc([S], BF16)
    BKT = P.bufs(8, "KT", dma=True)
    Bsel = Buf("sel", P.dsem())
    P.dma("pool", KT[64:128, :], g["sel_d"], writes=[Bsel])
    VA = A.alloc([128, 66], BF16)
    BVA = P.bufs(8, "VA", dma=True)
    Bva1 = Buf("va1")
    P.op("dve", lambda e: e.memset(VA[:, :, 64:66], 1.0), writes=[Bva1])
    QA = [A.alloc([T], BF16) for _ in range(2)]; BQA = P.bufs(2, "QA", dma=True)
    KO = [A.alloc([T], BF16) for _ in range(2)]; BKO = P.bufs(2, "KO", dma=True)
    for i in range(2):
        P.op("dve", lambda e, i=i: e.memset(KO[i][64:128, :], 0.0), writes=[BKO[i]])
    VO = [A.alloc([16, 66], BF16) for _ in range(2)]; BVO = P.bufs(2, "VO", dma=True)
    for i in range(2):
        P.op("dve", lambda e, i=i: e.memset(VO[i][:, :, 64:66], 1.0), writes=[BVO[i]])
    cb = A.alloc([2, 256], BF16); Bcb = Buf("cb", P.dsem())
    P.dma("pool", cb.rearrange("p a b -> p (a b)"), g["cb_d"], writes=[Bcb])
    PT = [A.alloc([512], BF16) for _ in range(4)]; BPT = P.bufs(4, "PT")
    rec = A.alloc([512], F32); Brec = Buf("rec"); drec = P.dsem()
    recbc = A.alloc([512], F32); Brecbc = Buf("recbc", P.dsem())
    tmpo = A.alloc([512], F32); Btmpo = Buf("tmpo")
    atth = [A.alloc([T], BF16) for _ in range(2)]; Batth = P.bufs(2, "atth")
    rec_d = nc.dram_tensor("rec_d", [512], F32).ap()
    Brecd = Buf("recd")
    agk_o = g["agk_out"].rearrange("(r x) t -> x r t", r=8)
    agv_o = g["agv_out"].rearrange("(r i c p) f -> p i r c f", r=8, i=8, c=2, p=128)
    agv_l = g["agv_in"].rearrange("(n p) f -> p n f", p=128)
    VAv = VA.rearrange("p (i r c) f -> p i r c f", i=8, r=8, c=2)
    KTv = KT.rearrange("p (i r t) -> p i r t", i=8, r=8)
    ntile = 0
    NHEADS = int(os.environ.get('ATT_HEADS', NH))

    def emit_loads_small(h):
        hi = (h + int(os.environ.get('ATT_HI', 0))) % 2
        P.dma("sp", QA[hi][0:64, :], g["q_d"][h * 64:(h + 1) * 64, :], reads=[Bqd], writes=[BQA[hi]])
        P.dma("sp", QA[hi][64:128, :], g["mb_d"][h * 64:(h + 1) * 64, :], reads=[Bmbd], writes=[BQA[hi]])
        P.dma("sp", KO[hi][0:64, :], g["agk_in"][h * 64:(h + 1) * 64, :], reads=[Bagk], writes=[BKO[hi]])
        P.dma("sp", VO[hi][:, :, 0:64], agv_l[:, :, h * 64:(h + 1) * 64], reads=[Bagv], writes=[BVO[hi]])

    def emit_loads_oct(h, i):
        P.dma("sp", KTv[0:64, i, :, :], agk_o[h * 64:(h + 1) * 64, :, i * 256:(i + 1) * 256], reads=[Bagko, Bsel], writes=[BKT[i]])
        for c2 in range(2):
            P.dma("sp", VAv[:, i, :, c2, 0:64], agv_o[:, i, :, c2, h * 64:(h + 1) * 64], reads=[Bagvo, Bva1], writes=[BVA[i]])

    STEP = int(os.environ.get('ATT_STEP', 9))
    emit_loads_small(0)
    for i in range(8):
        emit_loads_oct(0, i)
    for h in range(NHEADS):
        hi = (h + int(os.environ.get('ATT_HI', 0))) % 2
        if h + 1 < NHEADS:
            emit_loads_small(h + 1)
        for b in range(4 if STEP >= 1 else 0):
            P.op("dve", lambda e, b=b: e.memset(pbank[b][:], 0.0), writes=[PB[b]])
        tiles = []
        for k in range(8):
            for kc in range(2):
                tiles.append(("own", k, kc, k * 256, (k + 1) * 256))
        for j in range(int(os.environ.get('ATT_J', 63))):
            kmin = (j + 1) // 8
            r, i = block_loc(j)
            cp = 8 * i + r
            for kc in range(2):
                for b in range(kmin // 2, 4):
                    lo = max(kmin, 2 * b)
                    tiles.append(("past", cp, kc, lo * 256, (2 * b + 2) * 256, i))
        if STEP < 2:
            tiles = []
        LAG = int(os.environ.get('ATT_LAG', 3))
        pend = []

        last_of = {}
        for ti, tl in enumerate(tiles):
            if tl[0] == "past":
                last_of[tl[5]] = ti
        last_at = {ti: i for i, ti in last_of.items()}

        def emit_pv(vl, pt, ab, o0, n_, rds, ti):
            P.op("pe", lambda e: e.matmul(pbank[ab][0:65, o0:o0 + n_], lhsT=vl, rhs=PT[pt][:, 0:n_], start=False, stop=True, skip_group_check=True),
                 reads=rds + [BPT[pt]], writes=[PB[ab]])
            if ti in last_at and h + 1 < NHEADS:
                emit_loads_oct(h + 1, last_at[ti])

        for ti, tl in enumerate(tiles):
            sb = 4 + ntile % 4
            pt = ntile % 4
            ntile += 1
            if tl[0] == "own":
                _, k, kc, c0, c1 = tl
                n_ = c1 - c0
                P.op("pe", lambda e, k=k, kc=kc, c0=c0, c1=c1, sb=sb, n_=n_: e.matmul(pbank[sb][:, 0:n_], lhsT=KO[hi][:, k * 256 + kc * 128:k * 256 + (kc + 1) * 128], rhs=QA[hi][:, c0:c1], start=True, stop=False),
                     reads=[BKO[hi], BQA[hi]], writes=[PB[sb]], signal=False)
                P.op("pe", lambda e, kc=kc, sb=sb, n_=n_: e.matmul(pbank[sb][:, 0:n_], lhsT=identb, rhs=cb[:, kc, :], start=False, stop=True),
                     reads=[Bident, Bcb], writes=[PB[sb]])
                vl = VO[hi][:, k * 2 + kc, 0:65]
                rds = [BVO[hi]]
            else:
                _, cp, kc, c0, c1, i = tl
                n_ = c1 - c0
                P.op("pe", lambda e, cp=cp, kc=kc, c0=c0, c1=c1, sb=sb, n_=n_: e.matmul(pbank[sb][:, 0:n_], lhsT=KT[:, cp * 256 + kc * 128:cp * 256 + (kc + 1) * 128], rhs=QA[hi][:, c0:c1], start=True, stop=True),
                     reads=[BKT[i], Bsel, BQA[hi]], writes=[PB[sb]])
                vl = VA[:, cp * 2 + kc, 0:65]
                rds = [BVA[i], Bva1]
            if os.environ.get('EXP_SKIP', '0') == '0':
              P.op("act", lambda e, sb=sb, pt=pt, n_=n_: e.activation(out=PT[pt][:, 0:n_], in_=pbank[sb][:, 0:n_], func=getattr(AF, os.environ.get('EXP_FUNC', 'Exp')), scale=float(os.environ.get('EXP_SCALE', 1.0))), reads=[PB[sb]], writes=[BPT[pt]])
            ab = c0 // 512
            o0 = c0 % 512
            if STEP < 3:
                continue
            pend.append((vl, pt, ab, o0, n_, rds, ti))
            if len(pend) > LAG:
                emit_pv(*pend.pop(0))
        while pend:
            emit_pv(*pend.pop(0))
        for b in range(int(os.environ.get('ATT_NORM', 4))):
            P.op("dve", lambda e, b=b: e.reciprocal(out=rec[64:65, :], in_=pbank[b][64:65, :]), reads=[PB[b]], writes=[Brec])
            P.dma("sp", rec_d, rec[64:65, :], reads=[Brec], writes=[Brecd], ds=drec)
            P.dma("sp", recbc[0:64, :], rec_d.partition_broadcast(64), reads=[Brecd], writes=[Brecbc])
            P.op("dve", lambda e, b=b: e.tensor_copy(out=tmpo[0:64, :], in_=pbank[b][0:64, :]), reads=[PB[b]], writes=[Btmpo])
            P.op("dve", lambda e, b=b: e.tensor_tensor(out=atth[hi][0:64, b * 512:(b + 1) * 512], in0=tmpo[0:64, :], in1=recbc[0:64, :], op=ALU.mult), reads=[Btmpo, Brecbc], writes=[Batth[hi]])
        P.dma("sp", attT[hi * 64:(hi + 1) * 64, h // 2, :], atth[hi][0:64, :], reads=[Batth[hi]], writes=[BattT[h // 2]])
    P.barrier()

    if UPTO == "C":
        return
    A.off = reg_off
    wst = [A.alloc([8, 128], BF16) for _ in range(3)]
    wao = A.alloc([8, 1024], BF16); Bwao = Buf("wao", P.dsem())
    wo = A.alloc([8, 1024], BF16); Bwo = Buf("wo", P.dsem())
    P.dma("pool", wao, g["wao_d"].rearrange("(c p) f -> p c f", p=128), writes=[Bwao])
    P.dma("pool", wo, g["wo_d"].rearrange("(c p) f -> p c f", p=128), writes=[Bwo])
    sga = [A.alloc([512], F32) for _ in range(2)]; Bsga = P.bufs(2, "sga")
    tmpa = [A.alloc([512], F32) for _ in range(2)]; Btmpa = P.bufs(2, "tmpa")
    L = g["alloc_ln"]()
    g["load_ln_params"](L, 1)
    nsg = 0
    for G in range(4):
        for dc in range(8):
            for c in range(8):
                P.op("pe", lambda e, c=c, dc=dc: e.matmul(pbank[0][:], lhsT=wao[:, c, dc * 128:(dc + 1) * 128], rhs=attT[:, c, G * 512:(G + 1) * 512], start=(c == 0), stop=(c == 7)),
                     reads=[Bwao, BattT[c]], writes=[PB[0]], signal=(c == 7))
            w = load_w(6144 + dc * 128)
            fm_mm(w, 1, G)
            si = nsg % 2
            nsg += 1
            P.op("act", lambda e, si=si: e.activation(out=sga[si], in_=pbank[1][:], func=AF.Sigmoid), reads=[PB[1]], writes=[Bsga[si]])
            P.op("dve", lambda e, si=si: e.tensor_tensor(out=tmpa[si], in0=pbank[0][:], in1=sga[si], op=ALU.mult), reads=[PB[0], Bsga[si]], writes=[Btmpa[si]])
            P.op("dve", lambda e, si=si, dc=dc: e.tensor_tensor(out=AT[:, dc, G * 512:(G + 1) * 512], in0=AT[:, dc, G * 512:(G + 1) * 512], in1=tmpa[si], op=ALU.add), reads=[Btmpa[si], BAT[G]], writes=[BAT[G]])
    for t in range(NT):
        yb = [2 + 3 * (t % 2), 3 + 3 * (t % 2)]
        for h in range(2):
            for c in range(8):
                P.op("pe", lambda e, c=c, h=h, t=t, bank=yb[h]: e.matmul(pbank[bank][:], lhsT=AT[:, c, t * 128:(t + 1) * 128], rhs=wo[:, c, h * 512:(h + 1) * 512], start=(c == 0), stop=(c == 7)),
                     reads=[BAT[t // 4], Bwo], writes=[PB[yb[h]]], signal=(c == 7))
        g["ln_tile"](L, t, yb, g["x1_d"], g["x2_d"] if g["stage"] > 2 else g["out_d"], 4 + 3 * (t % 2), final=(g["stage"] == 2))
    P.barrier()


def host_consts(c):
    blks = blocks_of(c)
    pos = np.concatenate([np.arange(b * 256, (b + 1) * 256) for b in blks]).astype(np.float64)
    inv = 500000.0 ** (-np.arange(0, 16, 2, dtype=np.float32) / 16.0)
    ang = pos.astype(np.float32)[:, None] * inv[None, :].astype(np.float32)
    cos, sin = np.cos(ang).T, np.sin(ang).T
    cosT = np.ones((128, T), np.float32)
    sinT = np.zeros((128, T), np.float32)
    for hh in range(2):
        cosT[hh * 64:hh * 64 + 8] = cos
        cosT[hh * 64 + 8:hh * 64 + 16] = cos
        sinT[hh * 64:hh * 64 + 8] = -sin
        sinT[hh * 64 + 8:hh * 64 + 16] = sin
    biasA = np.zeros((8, 64), np.float32)
    biasB = np.zeros((8, 64), np.float32)
    for k in range(8):
        own = blks[k]
        for j in range(64):
            r, i = block_loc(j)
            gp = r * 8 + i
            biasA[k, gp] = 0.0 if j < own else NEG
            biasB[k, gp] = -BIG if j < own else -2 * BIG
    biasA = np.broadcast_to(biasA.reshape(1, 512), (128, 512)).copy()
    biasB = np.broadcast_to(biasB.reshape(1, 512), (128, 512)).copy()
    return dict(cosT=cosT, sinT=sinT, biasA=biasA, biasB=biasB)


def shared_consts():
    bf = ml_dtypes.bfloat16
    pmat = np.zeros((128, 128), np.float32)
    for hh in range(2):
        for m in range(16):
            partner = m + 8 if m < 8 else m - 8
            pmat[hh * 64 + partner, hh * 64 + m] = 1.0
    s_idx = np.arange(128)
    trimask = (s_idx[:, None] <= s_idx[None, :]).astype(np.float32)
    cb = np.zeros((128, 2, 256), np.float32)
    for kc in range(2):
        key = kc * 128 + np.arange(128)
        cb[:, kc, :] = np.where(key[:, None] <= np.arange(256)[None, :], 0.0, -BIG)
    sel = np.zeros((64, S), np.float32)
    for j in range(64):
        r, i = block_loc(j)
        cp = 8 * i + r
        sel[r * 8 + i, cp * 256:(cp + 1) * 256] = 1.0
    return dict(pmat=pmat, trimask=trimask, cb=np.ascontiguousarray(cb.reshape(128, 512)), sel=sel,
                identb=np.eye(128, dtype=np.float32), identf=np.eye(128, dtype=np.float32))


_NC_CACHE = {}


def kernel(x, ffn1_w_gate, ffn1_w_up, ffn1_w_down, ln1_g, ln1_b, w_in,
           gm_ln_g, gm_ln_b, gm_w_s, gm_b_s, w_gm_out, w_att_out, w_o,
           ln2_g, ln2_b, ffn2_w_gate, ffn2_w_up, ffn2_w_down, ln3_g, ln3_b, _stage=3):
    f = lambda a: np.ascontiguousarray(np.asarray(a, dtype=np.float32))
    x = f(x)[0]
    shared = dict(
        wg1=f(ffn1_w_gate)[0], wu1=f(ffn1_w_up)[0], wd1=f(ffn1_w_down)[0], ln1g=f(ln1_g)[0], ln1b=f(ln1_b)[0],
        win=f(w_in)[0], gmg=f(gm_ln_g)[0], gmb=f(gm_ln_b)[0], gmws=f(gm_w_s)[0], gmbs=f(gm_b_s)[0].reshape(1024),
        wgmo=f(w_gm_out)[0], wao=f(w_att_out)[0], wo=f(w_o)[0], ln2g=f(ln2_g)[0], ln2b=f(ln2_b)[0],
        wg2=f(ffn2_w_gate)[0], wu2=f(ffn2_w_up)[0], wd2=f(ffn2_w_down)[0], ln3g=f(ln3_g)[0], ln3b=f(ln3_b)[0])
    shared.update(shared_consts())
    in_maps = []
    for c in range(NCORE):
        blks = blocks_of(c)
        xc = np.concatenate([x[b * 256:(b + 1) * 256] for b in blks], axis=0)
        m = dict(shared)
        m["x"] = np.ascontiguousarray(xc)
        m.update(host_consts(c))
        in_maps.append(m)
    if _stage not in _NC_CACHE:
        _NC_CACHE[_stage] = build_program(_stage)
    nc = _NC_CACHE[_stage]
    res = run_bass_kernel_spmd(nc, in_maps, core_ids=list(range(NCORE)))
    out = np.empty((1, S, D), np.float32)
    for c in range(NCORE):
        oc = res.results[c]["out"]
        for k, b in enumerate(blocks_of(c)):
            out[0, b * 256:(b + 1) * 256] = oc[k * 256:(k + 1) * 256]
    return out
```
